# Optimizing a Trainium2 kernel written in Bass

```python
import jax, jax.numpy as jnp
from jax import lax
import numpy as np

D_MODEL = 1024
BATCH = 8
SEQ = 2048
DEPTH = 1
DEC_BATCH = 128
DEC_SEQ = 1
PAST_LEN = 8192
PAGE_SIZE = 128

D_MIX = D_MODEL
D_REC = D_MIX // 2
N_REC_BLOCKS = 8
REC_BLOCK = D_REC // N_REC_BLOCKS
CONV_W = 4
LRU_C = 8.0
N_HEADS = 8
HEAD_DIM = 64
N_KV_HEADS = 2
GQA_GROUP = N_HEADS // N_KV_HEADS
D_ATTN = N_HEADS * HEAD_DIM
D_KV = N_KV_HEADS * HEAD_DIM
WINDOW = 128
ATTN_BLOCK = 128
D_FF = -(-8 * D_MODEL // (3 * 256)) * 256
D_IN = 2 * D_REC + D_ATTN + 2 * D_KV
IN_SPLITS = (D_REC, 2 * D_REC, 2 * D_REC + D_ATTN, 2 * D_REC + D_ATTN + D_KV)
EPS = 1e-6

kernel_name = 'hymba_style_rglru_swa_sink_decoder_step'


def _rmsnorm(x, g):
    xf = x.astype(jnp.float32)
    y = xf * lax.rsqrt(jnp.mean(xf * xf, axis=-1, keepdims=True) + EPS) * g.astype(jnp.float32)
    return y.astype(x.dtype)


def _causal_conv(x, buf, w, b):
    xp = jnp.concatenate([buf.astype(x.dtype), x], axis=1)
    T = x.shape[1]
    y = b + xp[:, 0:T] * w[0]
    for j in range(1, CONV_W):
        y = y + xp[:, j:j + T] * w[j]
    return y, xp[:, xp.shape[1] - (CONV_W - 1):]


def _block_diag(x, w, b):
    xb = x.reshape(x.shape[:-1] + (N_REC_BLOCKS, REC_BLOCK))
    y = jnp.einsum('btni,nij->btnj', xb, w.astype(jnp.float32))
    return y.reshape(x.shape) + b.astype(jnp.float32)


def _rglru(x, h0, gate_a_w, gate_a_b, gate_x_w, gate_x_b, lru_lambda):
    xf = x.astype(jnp.float32)
    r = jax.nn.sigmoid(_block_diag(xf, gate_a_w, gate_a_b))
    i = jax.nn.sigmoid(_block_diag(xf, gate_x_w, gate_x_b))
    log_a = -LRU_C * r * jax.nn.softplus(-lru_lambda.astype(jnp.float32))
    a = jnp.exp(log_a)
    u = jnp.sqrt(-jnp.expm1(2.0 * log_a)) * (i * xf)

    def step(h, au):
        a_t, u_t = au
        h = a_t * h + u_t
        return h, h

    h_last, hs = lax.scan(step, h0.astype(jnp.float32),
                          (jnp.swapaxes(a, 0, 1), jnp.swapaxes(u, 0, 1)))
    return jnp.swapaxes(hs, 0, 1), h_last


def _sink_attention(q, k, v, mask, sinks):
    s = jnp.einsum('bnqkgd,bnskd->bnkgqs', q.astype(jnp.float32), k.astype(jnp.float32)) * (HEAD_DIM ** -0.5)
    s = jnp.where(mask[None, :, None, None], s, -jnp.inf)
    sink = sinks.astype(jnp.float32).reshape(1, 1, N_KV_HEADS, GQA_GROUP, 1, 1)
    sink = jnp.broadcast_to(sink, s.shape[:-1] + (1,))
    p = jax.nn.softmax(jnp.concatenate([s, sink], axis=-1), axis=-1)[..., :-1]
    o = jnp.einsum('bnkgqs,bnskd->bnqkgd', p, v.astype(jnp.float32))
    return o.astype(q.dtype)


def _prompt_attention(q, k, v, sinks):
    B, T = q.shape[:2]
    nb = T // ATTN_BLOCK
    qb = q.reshape(B, nb, ATTN_BLOCK, N_KV_HEADS, GQA_GROUP, HEAD_DIM)
    kb = k.reshape(B, nb, ATTN_BLOCK, N_KV_HEADS, HEAD_DIM)
    vb = v.reshape(B, nb, ATTN_BLOCK, N_KV_HEADS, HEAD_DIM)

    def band(xb):
        prev = jnp.concatenate([jnp.zeros_like(xb[:, :1]), xb[:, :-1]], axis=1)
        return jnp.concatenate([prev, xb], axis=2)

    blk = jnp.arange(nb)[:, None, None] * ATTN_BLOCK
    qpos = blk + jnp.arange(ATTN_BLOCK)[None, :, None]
    kpos = blk - ATTN_BLOCK + jnp.arange(2 * ATTN_BLOCK)[None, None, :]
    mask = (kpos >= 0) & (kpos <= qpos) & (kpos >= qpos - WINDOW)
    o = _sink_attention(qb, band(kb), band(vb), mask, sinks)
    return o.reshape(B, T, D_ATTN)


def _sample_attention(q, k, v, k_buf, v_buf, sinks):
    B, T = q.shape[:2]
    Wb = k_buf.shape[1]
    k_all = jnp.concatenate([k_buf.astype(k.dtype), k], axis=1)
    v_all = jnp.concatenate([v_buf.astype(v.dtype), v], axis=1)
    qpos = Wb + jnp.arange(T)[:, None]
    kpos = jnp.arange(Wb + T)[None, :]
    mask = ((kpos <= qpos) & (kpos >= qpos - WINDOW))[None]
    o = _sink_attention(q[:, None], k_all[:, None], v_all[:, None], mask, sinks)
    return o.reshape(B, T, D_ATTN), k_all[:, T:], v_all[:, T:]


def _layer(x, conv_buf, h0, k_buf, v_buf, norm1_g, w_in, conv_w, conv_b, gate_a_w, gate_a_b,
           gate_x_w, gate_x_b, lru_lambda, attn_sinks, rec_norm_g, attn_norm_g, w_out,
           norm2_g, w_gate, w_up, w_down):
    B, T, _ = x.shape
    xn = _rmsnorm(x, norm1_g)
    z = jnp.einsum('btd,de->bte', xn, w_in)
    xr, gr, q, k, v = jnp.split(z, IN_SPLITS, axis=-1)
    if conv_buf is None:
        conv_buf = jnp.zeros((B, CONV_W - 1, D_REC), x.dtype)
        h0 = jnp.zeros((B, D_REC), jnp.float32)
    xc, new_conv = _causal_conv(xr, conv_buf, conv_w, conv_b)
    hs, h_last = _rglru(xc, h0, gate_a_w, gate_a_b, gate_x_w, gate_x_b, lru_lambda)
    rec = (hs * jax.nn.gelu(gr.astype(jnp.float32))).astype(x.dtype)
    q = q.reshape(B, T, N_KV_HEADS, GQA_GROUP, HEAD_DIM)
    k = k.reshape(B, T, N_KV_HEADS, HEAD_DIM)
    v = v.reshape(B, T, N_KV_HEADS, HEAD_DIM)
    if k_buf is None:
        att = _prompt_attention(q, k, v, attn_sinks)
        keep = min(WINDOW, T)
        new_k, new_v = k[:, T - keep:], v[:, T - keep:]
    else:
        att, new_k, new_v = _sample_attention(q, k, v, k_buf, v_buf, attn_sinks)
    mix = jnp.concatenate([_rmsnorm(rec, rec_norm_g), _rmsnorm(att, attn_norm_g)], axis=-1)
    h = x + jnp.einsum('bte,ed->btd', mix, w_out)
    hn = _rmsnorm(h, norm2_g)
    ff = jax.nn.silu(jnp.einsum('btd,df->btf', hn, w_gate)) * jnp.einsum('btd,df->btf', hn, w_up)
    y = h + jnp.einsum('btf,fd->btd', ff, w_down)
    return y, new_k, new_v, new_conv, h_last.astype(x.dtype)


def setup_inputs(seed: int = 0) -> dict:
    key = jax.random.key(seed)
    ks = jax.random.split(key, 24)
    f32 = jnp.float32
    w_buf = min(WINDOW, PAST_LEN)

    def nrm(k, shape, scale):
        return jax.random.normal(k, shape, f32) * scale

    def gain(k, shape):
        return 1.0 + 0.05 * jax.random.normal(k, shape, f32)

    a0 = jax.random.uniform(ks[14], (DEPTH, D_REC), f32, 0.9, 0.999)
    s = a0 ** (1.0 / LRU_C)
    lru_lambda = jnp.log(s) - jnp.log1p(-s)
    return {
        'x_prompt': nrm(ks[0], (BATCH, SEQ, D_MODEL), 1.0),
        'x_sample': nrm(ks[1], (DEC_BATCH, DEC_SEQ, D_MODEL), 1.0),
        'cache_k_win': nrm(ks[2], (DEPTH, DEC_BATCH, w_buf, N_KV_HEADS, HEAD_DIM), 1.0),
        'cache_v_win': nrm(ks[3], (DEPTH, DEC_BATCH, w_buf, N_KV_HEADS, HEAD_DIM), 1.0),
        'state_conv': nrm(ks[4], (DEPTH, DEC_BATCH, CONV_W - 1, D_REC), 1.0),
        'state_h': nrm(ks[5], (DEPTH, DEC_BATCH, D_REC), 0.5),
        'norm1_g': gain(ks[6], (DEPTH, D_MODEL)),
        'w_in': nrm(ks[7], (DEPTH, D_MODEL, D_IN), D_MODEL ** -0.5),
        'conv_w': nrm(ks[8], (DEPTH, CONV_W, D_REC), CONV_W ** -0.5),
        'conv_b': nrm(ks[9], (DEPTH, D_REC), 0.02),
        'gate_a_w': nrm(ks[10], (DEPTH, N_REC_BLOCKS, REC_BLOCK, REC_BLOCK), REC_BLOCK ** -0.5),
        'gate_a_b': nrm(ks[11], (DEPTH, D_REC), 0.02),
        'gate_x_w': nrm(ks[12], (DEPTH, N_REC_BLOCKS, REC_BLOCK, REC_BLOCK), REC_BLOCK ** -0.5),
        'gate_x_b': nrm(ks[13], (DEPTH, D_REC), 0.02),
        'lru_lambda': lru_lambda,
        'attn_sinks': nrm(ks[15], (DEPTH, N_HEADS), 0.5),
        'rec_norm_g': gain(ks[16], (DEPTH, D_REC)),
        'attn_norm_g': gain(ks[17], (DEPTH, D_ATTN)),
        'w_out': nrm(ks[18], (DEPTH, D_MIX, D_MODEL), D_MIX ** -0.5),
        'norm2_g': gain(ks[19], (DEPTH, D_MODEL)),
        'w_gate': nrm(ks[20], (DEPTH, D_MODEL, D_FF), D_MODEL ** -0.5),
        'w_up': nrm(ks[21], (DEPTH, D_MODEL, D_FF), D_MODEL ** -0.5),
        'w_down': nrm(ks[22], (DEPTH, D_FF, D_MODEL), D_FF ** -0.5),
        'final_norm_g': gain(ks[23], (D_MODEL,)),
    }


def reference(x_prompt, x_sample, cache_k_win, cache_v_win, state_conv, state_h,
              norm1_g, w_in, conv_w, conv_b, gate_a_w, gate_a_b, gate_x_w, gate_x_b,
              lru_lambda, attn_sinks, rec_norm_g, attn_norm_g, w_out, norm2_g,
              w_gate, w_up, w_down, final_norm_g):
    params = (norm1_g, w_in, conv_w, conv_b, gate_a_w, gate_a_b, gate_x_w, gate_x_b,
              lru_lambda, attn_sinks, rec_norm_g, attn_norm_g, w_out, norm2_g,
              w_gate, w_up, w_down)
    yp, ys = x_prompt, x_sample
    pk, pv, pc, ph = [], [], [], []
    sk, sv, sc, sh = [], [], [], []
    for l in range(DEPTH):
        p = [t[l] for t in params]
        yp, k1, v1, c1, h1 = _layer(yp, None, None, None, None, *p)
        ys, k2, v2, c2, h2 = _layer(ys, state_conv[l], state_h[l], cache_k_win[l], cache_v_win[l], *p)
        pk.append(k1); pv.append(v1); pc.append(c1); ph.append(h1)
        sk.append(k2); sv.append(v2); sc.append(c2); sh.append(h2)
    y_prompt = _rmsnorm(yp, final_norm_g)
    y_sample = _rmsnorm(ys, final_norm_g)
    return (y_prompt, y_sample, jnp.stack(pk), jnp.stack(pv), jnp.stack(pc), jnp.stack(ph),
            jnp.stack(sk), jnp.stack(sv), jnp.stack(sc), jnp.stack(sh))
```

```python
from contextlib import ExitStack
import numpy as np
import concourse.bass as bass
import concourse.mybir as mybir
from concourse.bass_utils import run_bass_kernel_spmd

F32 = mybir.dt.float32
BF16 = mybir.dt.bfloat16
AF = mybir.ActivationFunctionType
ALU = mybir.AluOpType
AX = mybir.AxisListType

NCORES = 8
D = 1024
SEQ = 2048
NSMP = 16
DFF = 2816
NF = 22
EPS = 1e-6
TT = 512
NEG = -30000.0
STAGE = 99
NDUMMY = 0
SKIP = ''


class _Stop(Exception):
    pass


def _gate(n):
    if STAGE < n:
        raise _Stop()


class Prog:
    ENGS = (("pe", "tensor"), ("act", "scalar"), ("dve", "vector"), ("pool", "gpsimd"), ("sp", "sync"))

    def __init__(self, nc, ndma=24):
        self.nc = nc
        self.ops = {e: [] for e, _ in self.ENGS}
        self.cnt = {e: 0 for e, _ in self.ENGS}
        self.lastw = {}
        self.readers = {}
        self.waited = {e: {} for e, _ in self.ENGS}
        self.ndma = ndma
        self.dma_tot = [0] * ndma
        self.dma_rr = 0
        self.dma_rr2 = 0

    def _deps(self, eng, reads, writes):
        need = {}

        def add(tok):
            if tok is None:
                return
            k, v = tok
            if k == eng and eng == "pe":
                return
            if need.get(k, 0) < v:
                need[k] = v

        for b in reads:
            add(self.lastw.get(b))
        for b in writes:
            add(self.lastw.get(b))
            for t in self.readers.get(b, {}).items():
                add(t)
        waits = []
        for k, v in need.items():
            if isinstance(k, str):
                assert v <= self.cnt[k], f"wait on future token {k} {v} > {self.cnt[k]} (open PE group?)"
            if self.waited[eng].get(k, 0) >= v:
                continue
            self.waited[eng][k] = v
            waits.append((k, v))
        return waits

    def _reg(self, tok, reads, writes):
        k, v = tok
        for b in reads:
            d = self.readers.setdefault(b, {})
            if d.get(k, 0) < v:
                d[k] = v
        for b in writes:
            self.lastw[b] = tok
            self.readers[b] = {}

    def op(self, eng, fn, r=(), w=(), inc=True):
        r = list(r); w = list(w)
        for b in r:
            if isinstance(b, str) and (b.startswith("pf") or b.startswith("pb")) and b[2:].isdigit() and b not in w:
                w.append(b)
        waits = self._deps(eng, r, w)
        if inc:
            self.cnt[eng] += 1
            tok = (eng, self.cnt[eng])
        else:
            tok = (eng, self.cnt[eng] + 1)
        self._reg(tok, r, w)
        self.ops[eng].append((fn, waits, (eng, 1) if inc else None))

    def dma(self, fn, r=(), w=(), q="sp"):
        waits = self._deps(q, r, w)
        nsp = self.ndma - 16
        if q == "sp":
            j = self.dma_rr % nsp
            self.dma_rr += 1
        else:
            j = nsp + self.dma_rr2 % 16
            self.dma_rr2 += 1
        prev = self.dma_tot[j]
        key = ("dma", j)
        if prev > 0 and self.waited[q].get(key, 0) < prev:
            waits.append((key, prev))
            self.waited[q][key] = prev
        self.dma_tot[j] += 16
        tok = (key, self.dma_tot[j])
        self._reg(tok, r, w)
        self.ops[q].append((fn, waits, (key, 16)))

    def finish(self, q="sp"):
        waits = []
        for j in range(self.ndma):
            if self.dma_tot[j] > 0:
                waits.append((("dma", j), self.dma_tot[j]))
        self.ops[q].append((None, waits, None))

    def emit(self):
        nc = self.nc
        with ExitStack() as st:
            sems = {}
            for e, _ in self.ENGS:
                sems[e] = st.enter_context(nc.semaphore("s_" + e))
            for j in range(self.ndma):
                sems[("dma", j)] = st.enter_context(nc.semaphore(f"s_dma{j}"))
            block = st.enter_context(nc.Block())
            for e, attr in self.ENGS:
                ops = self.ops[e]

                def body(eng, ops=ops):
                    for fn, waits, inc in ops:
                        for k, v in waits:
                            eng.wait_ge(sems[k], v)
                        if fn is None:
                            continue
                        ins = fn(eng)
                        if inc is not None:
                            ins.then_inc(sems[inc[0]], inc[1])

                getattr(block, attr)(body)


def build_program():
    nc = bass.Bass("TRN2", target_bir_lowering=False)
    P = Prog(nc, ndma=56)

    def din(name, shape, dt=F32):
        return nc.dram_tensor(name, list(shape), dt, kind="ExternalInput").ap()

    def dout(name, shape):
        return nc.dram_tensor(name, list(shape), F32, kind="ExternalOutput").ap()

    xp_d = din("xp", [SEQ, D]); xs_d = din("xs", [NSMP, D])
    ck_d = din("ck", [NSMP, 128, 128]); cv_d = din("cv", [NSMP, 128, 128])
    sc_d = din("sc", [NSMP, 1536]); sh_d = din("sh", [NSMP, 512])
    win_d = din("w_in", [D, 1792]); wout_d = din("w_out", [D, D])
    wg_d = din("w_gate", [D, DFF]); wu_d = din("w_up", [D, DFF]); wdn_d = din("w_down", [DFF, D])
    ga_d = din("gate_a_w", [8, 64, 64]); gx_d = din("gate_x_w", [8, 64, 64])
    cols_d = din("cols", [128, 48])
    gb_d = din("gbc", [3, 128, D])
    sinkb_d = din("sinkb", [128, 8])
    sinkc_d = din("sinkc", [128, 1])
    ident_d = din("ident", [128, 128])
    rep_d = din("rep", [128, 8, 128])
    mask_d = din("mask", [128, 2, 512])

    yp_d = dout("y_p", [SEQ, D]); ys_d = dout("y_s", [NSMP, D])
    pk_d = dout("p_k", [128, 128]); pv_d = dout("p_v", [128, 128])
    pc_d = dout("p_conv", [3, 512]); ph_d = dout("p_h", [512])
    sk_d = dout("s_k", [NSMP, 128, 128]); sv_d = dout("s_v", [NSMP, 128, 128])
    sco_d = dout("s_conv", [NSMP, 3, 512]); sho_d = dout("s_h", [NSMP, 512])

    wgu_s = nc.dram_tensor("wgu_scr", [NF, 128, 2, 8, 128], BF16, kind="Internal").ap()
    wd_s = nc.dram_tensor("wd_scr", [2, NF, 128, 512], BF16, kind="Internal").ap()
    q_s = nc.dram_tensor("q_scr", [NSMP, 512], F32, kind="Internal").ap()
    kv_s = nc.dram_tensor("kv_scr", [NSMP, 384], F32, kind="Internal").ap()
    o_s = nc.dram_tensor("o_scr", [NSMP, 512], F32, kind="Internal").ap()

    st = ExitStack()

    def sb(name, shape, dt=F32):
        return st.enter_context(nc.sbuf_tensor(name, list(shape), dt))

    def psum(name, shape, dt=F32):
        return st.enter_context(nc.psum_tensor(name, list(shape), dt))

    WINX = sb("WINX", [128, 8, 1920], BF16)
    WOUT = sb("WOUT", [128, 8, D], BF16)
    NWGU, NWD = 3, 3
    WGU = [sb(f"WGU{i}", [128, 2, 8, 128], BF16) for i in range(NWGU)]
    WD = [sb(f"WD{i}", [128, 512], BF16) for i in range(NWD)]
    BDA = sb("BDA", [128, 4, 128], BF16); BDX = sb("BDX", [128, 4, 128], BF16)
    IDF = sb("IDF", [128, 128]); IDB = sb("IDB", [128, 128], BF16); ONES = sb("ONES", [128, 2], BF16)
    MASK = sb("MASK", [128, 2, 512], BF16)
    GB = sb("GB", [128, 3, D])
    COLS = sb("COLS", [128, 48]); DER = sb("DER", [128, 32])
    SINKB = sb("SINKB", [128, 8]); SINKC = sb("SINKC", [128, 1]); NSINKB = sb("NSINKB", [128, 9])
    RES = sb("RES", [128, 4, D])
    XN = [sb(f"XN{i}", [128, D], BF16) for i in range(2)]
    XT = sb("XT", [128, 8, TT], BF16)
    SCR = sb("SCR", [128, 17408], BF16)

    def carve32(off_kb, ncols):
        a = int(off_kb * 512)
        return SCR[:, a:a + 2 * ncols].bitcast(F32)

    XR = [carve32(0, 515), carve32(16, 515)]
    XC = [carve32(2.25, 512), carve32(18.25, 512)]
    TR = [carve32(4.25, 512), carve32(20.25, 512)]
    TI = [carve32(6.25, 512), carve32(22.25, 512)]
    AA = [carve32(8.25, 512), carve32(24.25, 512)]
    GX = [carve32(10.25, 512), carve32(26.25, 512)]
    GX2 = [carve32(12.25, 512), carve32(28.25, 512)]
    XCB = [SCR[:, int(14.25 * 512):int(14.25 * 512) + 512], SCR[:, int(30.25 * 512):int(30.25 * 512) + 512]]
    RSQ = [SCR[:, int(15.25 * 512):int(15.25 * 512) + 384], SCR[:, int(31.25 * 512):int(31.25 * 512) + 384]]
    FFT = SCR[:, 0:NF * 512]
    SIL = [carve32(22, 512), carve32(24, 512)]
    STOK = SCR[0:NSMP, 8192:12288].bitcast(F32)
    RGN = ["xr", "xc", "tr", "ti", "aa", "gx", "gx2", "xcb"]
    KSET = [[n + "0" for n in RGN], [n + "1" for n in RGN]]
    STK = ["stok"] + KSET[1]

    HSB = [sb(f"HS{i}", [128, TT]) for i in range(2)]
    HCAR = sb("HCAR", [128, 4])
    RSQB = [sb(f"RSQB{i}", [128, TT], BF16) for i in range(2)]
    HALO = sb("HALO", [128, 4, 3])
    QB = sb("QB", [128, 4, TT], BF16)
    KT = sb("KT", [128, 2, 2, 128 + TT], BF16)
    KF = sb("KF", [128, TT]); VF = sb("VF", [128, TT])
    VTOK = sb("VTOK", [128, 5, 2, 2, 128], BF16)
    KVOUT = sb("KVOUT", [128, 2, 128])
    REC = sb("REC", [128, 4, TT], BF16); ATT = sb("ATT", [128, 4, TT], BF16)
    ASQ = [sb(f"ASQ{i}", [128, TT], BF16) for i in range(2)]
    P32 = [sb(f"P32_{i}", [128, 512]) for i in range(2)]
    BDS = P32[0][:, :].rearrange("p (c j) -> p c j", c=4)
    PB16 = [sb(f"PB16_{i}", [128, 512], BF16) for i in range(2)]
    PTS = [sb(f"PTS{i}", [128, 512], BF16) for i in range(2)]
    SM = sb("SM", [128, 64])
    SCT = sb("SCT", [128, 12, NSMP]); H0T = sb("H0T", [128, 4, NSMP])
    XSTG = sb("XSTG", [128, 2, D])
    QS = sb("QS", [128, 64]); KNEW = sb("KNEW", [128, 64]); VNEW = sb("VNEW", [128, 64])
    SCS = sb("SCS", [128, 132]); PS_ = sb("PS_", [128, 132]); OS = sb("OS", [128, 64]); OPART = sb("OPART", [128, 64])
    ATOK = sb("ATOK", [NSMP, 512])
    PCH = sb("PCH", [3, 512]); PHT = sb("PHT", [1, 512])
    SRG = sb("SRG", [128, 2, 7, 24]); SXCB = sb("SXCB", [128, 2, NSMP], BF16)
    RECS = sb("RECS", [128, 4, NSMP], BF16); ATTS = sb("ATTS", [128, 4, NSMP], BF16)
    RSQS = sb("RSQS", [128, 2, NSMP], BF16); ASQS = sb("ASQS", [128, 2, NSMP], BF16)
    XSMP = sb("XSMP", [NSMP, D])
    XTS = sb("XTS", [128, 8, NSMP], BF16)
    FFTS = sb("FFTS", [128, NF, NSMP], BF16); SILS = sb("SILS", [128, NSMP])

    PBK = [psum(f"PBK{i}", [128, 1024], BF16) for i in range(2)]
    PFK = [psum(f"PFK{i}", [128, 512]) for i in range(6)]
    rr = {"pf": 0, "pb": 0, "wgu": 0, "wd": 0}

    def pf(n=2):
        if n == 2:
            i = 2 + rr["pf"] % 2
        else:
            i = rr["pf"] % 4
        rr["pf"] += 1
        return PFK[i], f"pf{i}"

    def pb():
        i = rr["pb"]; rr["pb"] = (i + 1) % 2
        return PBK[i], f"pb{i}"

    op, dma = P.op, P.dma

    dma(lambda e: e.dma_start(out=COLS[:], in_=cols_d), w=["cols"])
    dma(lambda e: e.dma_start(out=IDF[:], in_=ident_d), w=["idf"])
    dma(lambda e: e.dma_start(out=MASK[:], in_=mask_d), w=["mask"], q="pool")
    dma(lambda e: e.dma_start(out=SINKB[:], in_=sinkb_d), w=["sinkb"])
    dma(lambda e: e.dma_start(out=SINKC[:], in_=sinkc_d), w=["sinkc"])
    dma(lambda e: e.dma_start(out=GB[:], in_=gb_d.rearrange("g p d -> p g d")), w=["gb"])
    dma(lambda e: e.dma_start(out=XSMP[:, :], in_=xs_d), w=["xsmp"])
    dma(lambda e: e.dma_start(out=STOK[:, 0:1536], in_=sc_d), w=STK)
    dma(lambda e: e.dma_start(out=STOK[:, 1536:2048], in_=sh_d), w=STK)
    KVB = RES[:, :, :].rearrange("p s d -> p (s d)").bitcast(BF16)
    KVK = ["res0", "res1", "res2", "res3"]
    REPB = [PB16[i][:, 0:512].rearrange("p (g m) -> p g m", g=4) for i in range(2)]
    dma(lambda e: e.dma_start(out=REPB[0], in_=rep_d[:, 0:4, :]), w=["pb16_0"], q="pool")
    dma(lambda e: e.dma_start(out=REPB[1], in_=rep_d[:, 4:8, :]), w=["pb16_1"], q="pool")
    op("pool", lambda e: e.memset(KVB, 0.0), w=KVK)
    WST = [WGU[i][:].rearrange("p a k j -> p (a k j)").bitcast(F32) for i in range(3)]
    stg_rr = [0]

    def stage_cast(eng, dst, src, sk_, wk):
        if eng == "dve":
            op("dve", lambda e: e.tensor_copy(dst, src), r=[sk_], w=[wk])
        else:
            op("act", lambda e: e.copy(dst, src), r=[sk_], w=[wk])

    def stage_win():
        for k in range(8):
            for half in range(2):
                i = stg_rr[0] % 3; stg_rr[0] += 1
                stg, sk_ = WST[i], f"wgu{i}"
                dma(lambda e, k=k, half=half, stg=stg: e.dma_start(out=stg[:, 0:896], in_=win_d[k * 128:(k + 1) * 128, half * 896:(half + 1) * 896]), w=[sk_])
                eng = "dve" if (2 * k + half) % 2 == 0 else "act"
                if half == 0:
                    stage_cast(eng, WINX[:, k, 0:896], stg[:, 0:896], sk_, "winx")
                else:
                    stage_cast(eng, WINX[:, k, 896:1536], stg[:, 0:640], sk_, "winx")
                    oth = "act" if eng == "dve" else "dve"
                    for (a, lo, n) in ((1536, 640, 64), (1600, 640, 64), (1664, 704, 64), (1728, 704, 64), (1792, 768, 128)):
                        stage_cast(oth, WINX[:, k, a:a + n], stg[:, lo:lo + n], sk_, "winx")

    def stage_wout():
        for k in range(8):
            i = stg_rr[0] % 3; stg_rr[0] += 1
            stg, sk_ = WST[i], f"wgu{i}"
            dma(lambda e, k=k, stg=stg: e.dma_start(out=stg[:, 0:D], in_=wout_d[k * 128:(k + 1) * 128, :]), w=[sk_])
            stage_cast("dve" if k % 2 == 0 else "act", WOUT[:, k, :], stg[:, 0:D], sk_, "wout")

    stage_win()
    for r, (src_d, k0) in enumerate(((ck_d, 0), (ck_d, 64), (cv_d, 0), (cv_d, 64))):
        dma(lambda e, r=r, src_d=src_d, k0=k0: e.dma_start(out=KVB[r * NSMP:(r + 1) * NSMP, :], in_=src_d[:, k0:k0 + 64, :].rearrange("b k e -> b (k e)")),
            r=["winx"], w=KVK, q="pool")
    op("dve", lambda e: e.memset(BDS, 0.0), w=["bds", "p32_0"])
    for (src, dst, nm) in ((ga_d, BDA, "bda"), (gx_d, BDX, "bdx")):
        for h in range(2):
            dma(lambda e, src=src, h=h: e.dma_start(out=BDS[h * 64:(h + 1) * 64, :, h * 64:(h + 1) * 64],
                                                    in_=src[h:8:2].rearrange("c i j -> i c j")), r=[], w=["bds"])
        op("dve", lambda e, dst=dst: e.tensor_copy(dst[:], BDS), r=["bds"], w=[nm, "p32_0"])
    op("dve", lambda e: e.tensor_copy(IDB[:], IDF[:]), r=["idf"], w=["idb"])
    op("dve", lambda e: e.memset(ONES[:], 1.0), w=["ones"])
    op("dve", lambda e: e.tensor_scalar(NSINKB[:, 0:8], SINKB[:, :], -1.0, None, ALU.mult), r=["sinkb"], w=["nsinkb"])
    op("dve", lambda e: e.tensor_scalar(NSINKB[:, 8:9], SINKC[:, :], -1.0, None, ALU.mult), r=["sinkc", "nsinkb"], w=["nsinkb"])
    op("pool", lambda e: e.memset(KT[:], 0.0), w=["kt"])
    op("pool", lambda e: e.memset(VTOK[:], 0.0), w=["vtok"])
    op("pool", lambda e: e.memset(HALO[:], 0.0), w=["halo"])
    op("pool", lambda e: e.memset(HCAR[:], 0.0), w=["hcar"])
    op("dve", lambda e: e.tensor_scalar(DER[:, 0:8], COLS[:, 20:28], 0.5, None, ALU.mult), r=["cols"], w=["der"])
    op("act", lambda e: e.activation(DER[:, 20:24], COLS[:, 28:32], AF.Exp, scale=-1.0), r=["cols", "der"], w=["der"])
    op("act", lambda e: e.activation(DER[:, 24:28], DER[:, 20:24], AF.Ln, bias=1.0, scale=1.0), r=["der"], w=["der"])
    op("dve", lambda e: e.tensor_scalar(DER[:, 8:12], DER[:, 24:28], -4.0, None, ALU.mult), r=["der"], w=["der"])
    op("dve", lambda e: e.tensor_scalar(DER[:, 12:16], DER[:, 24:28], -8.0, None, ALU.mult), r=["der"], w=["der"])
    op("dve", lambda e: e.memset(DER[:, 16:17], EPS), r=["der"], w=["der"])
    op("dve", lambda e: e.memset(DER[:, 17:18], 4.0 * EPS), r=["der"], w=["der"])
    op("dve", lambda e: e.memset(DER[:, 18:19], 0.25), r=["der"], w=["der"])
    op("dve", lambda e: e.memset(DER[:, 19:20], 0.0), r=["der"], w=["der"])
    EPSC, EPS4C, QUARTC = DER[:, 16:17], DER[:, 17:18], DER[:, 18:19]
    for _i in range(SKIP.count('e')):
        op("act", lambda e: e.copy(SM[:, 60:61], DER[:, 19:20]), r=["der"], w=["dummy"])
    for _i in range(SKIP.count('f')):
        op("dve", lambda e: e.tensor_copy(SM[:, 61:62], DER[:, 19:20]), r=["der"], w=["dummy2"])
    wgv = wg_d.rearrange("(k p) (f j) -> f p k j", p=128, j=128)
    wuv = wu_d.rearrange("(k p) (f j) -> f p k j", p=128, j=128)
    wdv = wdn_d.rearrange("(f p) (h j) -> h f p j", p=128, j=512)
    for f in range(NF):
        dma(lambda e, f=f: e.dma_start(out=wgu_s[f, :, 0], in_=wgv[f]), r=(["winx"] if f == 0 else []), w=[f"wgud{f}"], q="pool")
        dma(lambda e, f=f: e.dma_start(out=wgu_s[f, :, 1], in_=wuv[f]), w=[f"wgud{f}"], q="pool")
    for h in range(2):
        for f0 in (0, 11):
            dma(lambda e, h=h, f0=f0: e.dma_start(out=wd_s[h, f0:f0 + 11], in_=wdv[h, f0:f0 + 11]),
                w=[f"wdd{h}_{f}" for f in range(f0, f0 + 11)], q="pool")

    def RS(smp, s):
        return XSMP[0:NSMP, :] if smp else RES[:, s, :]

    def RK(smp, s):
        return "xsmp" if smp else f"res{s}"

    def rstd_cols(dst_cols, src_ap, src_keys, scale, eps_col, R, n, tag):
        op("act", lambda e: e.activation(dst_cols, src_ap, AF.Sqrt, bias=eps_col[0:R], scale=scale), r=src_keys + ["der"], w=[tag])
        op("dve", lambda e: e.reciprocal(dst_cols, dst_cols), r=[tag], w=[tag])

    def norm_transpose(nsub, R, gidx, ss_off, rs_off, tag, smp=False, xdst=None):
        xdst = XT if xdst is None else xdst
        xk = "xt" if xdst is XT else "xts"
        for s in range(nsub):
            op("act", lambda e, s=s: e.activation(XN[s % 2][0:R, :], RS(smp, s), AF.Square, accum_out=SM[0:R, ss_off + s:ss_off + s + 1]),
               r=[RK(smp, s)], w=[f"xn{s%2}", f"ss{tag}{s}"])
        rstd_cols(SM[0:R, rs_off:rs_off + nsub], SM[0:R, ss_off:ss_off + nsub], [f"ss{tag}{s}" for s in range(nsub)],
                  1.0 / D, EPSC, R, nsub, f"rs{tag}")
        for s in range(nsub):
            op("dve", lambda e, s=s: e.scalar_tensor_tensor(XN[s % 2][0:R, :], RS(smp, s), SM[0:R, rs_off + s:rs_off + s + 1],
                                                            GB[0:R, gidx, :], ALU.mult, ALU.mult),
               r=[RK(smp, s), f"rs{tag}", "gb"], w=[f"xn{s%2}"])
            pbt, pbk = pb()
            for k in range(8):
                op("pe", lambda e, s=s, k=k, pbt=pbt: e.transpose(pbt[:, k * 128:k * 128 + R], XN[s % 2][0:R, k * 128:(k + 1) * 128], IDB[0:R, 0:R]),
                   r=[f"xn{s%2}", "idb"], w=[pbk], inc=(k == 7))
            op("act", lambda e, s=s, pbt=pbt: e.copy(xdst[:, :, s * R:(s + 1) * R], pbt[:, :].rearrange("p (k t) -> p k t", k=8)[:, :, 0:R]),
               r=[pbk], w=[xk])

    CH = {"xr": lambda c: c, "gr": lambda c: 4 + c, "q": lambda c: 8 + c, "kd": lambda g: 12 + g, "v": lambda _: 14}

    def win_chunk(ch, T):
        pft, pfk = pf()
        for k in range(8):
            op("pe", lambda e, k=k, pft=pft: e.matmul(pft[:, 0:T], WINX[:, k, ch * 128:(ch + 1) * 128], XT[:, k, 0:T],
                                                       start=(k == 0), stop=(k == 7)),
               r=["winx", "xt"], w=[pfk], inc=(k == 7))
        return pft, pfk

    def run(gen):
        for _ in gen:
            pass

    def mix(*gens, weights=None):
        gens = [g for g in gens if g is not None]
        wts = dict(zip(gens, weights or [1] * len(gens)))
        while gens:
            for g in list(gens):
                for _ in range(wts[g]):
                    try:
                        next(g)
                    except StopIteration:
                        gens.remove(g)
                        break

    PSTAT, PSTK = PFK[5], "pf5"
    PO, PKO = PFK[4], "pf4"

    def tile_geom(is_sample):
        return (NSMP, NSMP, 1) if is_sample else (TT, 128, 4)

    def load_norm1(ti, is_sample, do_load=True):
        T, R, nsub = tile_geom(is_sample)
        xsrc = xs_d if is_sample else xp_d[ti * TT:(ti + 1) * TT, :]
        if do_load:
            for s in range(nsub):
                dma(lambda e, s=s: e.dma_start(out=RS(is_sample, s), in_=xsrc[s * R:(s + 1) * R, :]), w=[RK(is_sample, s)])
        norm_transpose(nsub, R, 0, 0, 4, "a", smp=is_sample)

    def prefetch_norm1(ti):
        R = 128
        xsrc = xp_d[ti * TT:(ti + 1) * TT, :]
        for pair in range(2):
            for q in range(2):
                s = pair * 2 + q
                dma(lambda e, s=s, q=q: e.dma_start(out=XSTG[:, q, :], in_=xsrc[s * R:(s + 1) * R, :]), w=[f"xstg{q}"])
                op("act", lambda e, s=s, q=q: e.activation(XN[q][:, :], XSTG[:, q, :], AF.Square, accum_out=SM[:, s:s + 1]),
                   r=[f"xstg{q}"], w=[f"xn{q}", f"ssa{s}"])
            rstd_cols(SM[:, 4 + 2 * pair:6 + 2 * pair], SM[:, 2 * pair:2 * pair + 2], [f"ssa{2 * pair}", f"ssa{2 * pair + 1}"], 1.0 / D, EPSC, R, 2, "rsa")
            for q in range(2):
                s = pair * 2 + q
                op("dve", lambda e, s=s, q=q: e.scalar_tensor_tensor(XN[q][:, :], XSTG[:, q, :], SM[:, 4 + s:5 + s], GB[:, 0, :], ALU.mult, ALU.mult),
                   r=[f"xstg{q}", "rsa", "gb"], w=[f"xn{q}"])
                pbt, pbk = pb()
                for k in range(8):
                    op("pe", lambda e, q=q, k=k, pbt=pbt: e.transpose(pbt[:, k * 128:(k + 1) * 128], XN[q][:, k * 128:(k + 1) * 128], IDB[:, :]),
                       r=[f"xn{q}", "idb"], w=[pbk], inc=(k == 7))
                op("act", lambda e, s=s, pbt=pbt: e.copy(XT[:, :, s * R:(s + 1) * R], pbt[:, :].rearrange("p (k t) -> p k t", k=8)), r=[pbk], w=["xt"])
                yield

    def residual_load(ti):
        xsrc = xp_d[ti * TT:(ti + 1) * TT, :]
        for s in range(4):
            dma(lambda e, s=s: e.dma_start(out=RES[:, s, :], in_=xsrc[s * 128:(s + 1) * 128, :]), w=[f"res{s}"])

    def sample_state():
        for half in range(2):
            pft, pfk = pf()
            for i in range(8):
                op("pe", lambda e, i=i, half=half, pft=pft: e.transpose(pft[:, i * NSMP:(i + 1) * NSMP],
                                                                       STOK[:, (half * 8 + i) * 128:(half * 8 + i + 1) * 128], IDF[0:NSMP, 0:NSMP]),
                   r=STK + ["idf"], w=[pfk], inc=(i == 7))
            if half == 0:
                op("act", lambda e, pft=pft: e.copy(SCT[:, 0:8, :], pft[:, 0:8 * NSMP].rearrange("p (a b) -> p a b", b=NSMP)), r=[pfk], w=["sct"])
            else:
                op("act", lambda e, pft=pft: e.copy(SCT[:, 8:12, :], pft[:, 0:4 * NSMP].rearrange("p (a b) -> p a b", b=NSMP)), r=[pfk], w=["sct"])
                op("act", lambda e, pft=pft: e.copy(H0T[:, :, :], pft[:, 4 * NSMP:8 * NSMP].rearrange("p (a b) -> p a b", b=NSMP)), r=[pfk], w=["h0t"])

    def pool_or_dve(ti):
        return "dve"

    def rg_front(ti, c, is_sample):
        T, R, nsub = tile_geom(is_sample)
        last_prompt = (not is_sample) and ti == 3
        S_ = c % 2
        if is_sample:
            kxr_, kxc, ktr, kti, kaa, kgx, kgx2, kxcb = [n + f"s{S_}" for n in RGN]
            XR_, XC_, TR_, TI_, AA_, GX_, GX2_ = [SRG[:, S_, i, :] for i in range(7)]
            XCB_ = SXCB[:, S_, :]
            RSQ_, rsqk = RSQS[:, S_, :], f"rsqs{S_}"
            REC_, reck = RECS[:, c, :], "recs"
            pcol = 32 + c
        else:
            kxr_, kxc, ktr, kti, kaa, kgx, kgx2, kxcb = KSET[S_]
            XR_, XC_, TR_, TI_, AA_, GX_, GX2_, XCB_ = XR[S_], XC[S_], TR[S_], TI[S_], AA[S_], GX[S_], GX2[S_], XCB[S_]
            RSQ_, rsqk = RSQB[S_], f"rsq{S_}"
            REC_, reck = REC[:, c, :], "rec"
            pcol = None
        pxr, kxr = win_chunk(CH["xr"](c), T)
        op("act", lambda e: e.copy(XR_[:, 3:3 + T], pxr[:, 0:T]), r=[kxr], w=[kxr_])
        yield
        pgr, kgr = win_chunk(CH["gr"](c), T)
        op("act", lambda e: e.copy(GX_[:, 0:T], pgr[:, 0:T]), r=[kgr], w=[kgx])
        op("act", lambda e: e.activation(GX2_[:, 0:T], pgr[:, 0:T], AF.Square), r=[kgr], w=[kgx2])
        yield
        cw = lambda j: COLS[:, 4 * c + j:4 * c + j + 1]
        if not is_sample:
            op("dve", lambda e: e.tensor_copy(XR_[:, 0:3], HALO[:, c, :]), r=["halo"], w=[kxr_])
            srcs = [XR_[:, j:j + T] for j in range(4)]
            skeys = [kxr_]
        else:
            srcs = [SCT[:, j * 4 + c, :] for j in range(3)] + [XR_[:, 3:3 + T]]
            skeys = [kxr_, "sct"]
        op("dve", lambda e: e.tensor_scalar(XC_[:, 0:T], srcs[0], cw(0), COLS[:, 16 + c:17 + c], ALU.mult, ALU.add),
           r=skeys + ["cols"], w=[kxc])
        yield
        for j in range(1, 4):
            op("dve", lambda e, j=j: e.scalar_tensor_tensor(XC_[:, 0:T], srcs[j], cw(j), XC_[:, 0:T], ALU.mult, ALU.add),
               r=skeys + ["cols", kxc], w=[kxc])
            yield
        if not is_sample:
            op("dve", lambda e: e.tensor_copy(HALO[:, c, :], XR_[:, T:T + 3]), r=[kxr_], w=["halo"])
            if last_prompt:
                pft, pfk = pf()
                op("pe", lambda e: e.transpose(pft[0:3, 0:128], XR_[:, T:T + 3], IDF[:, :]), r=[kxr_, "idf"], w=[pfk])
                op("act", lambda e: e.copy(PCH[0:3, c * 128:(c + 1) * 128], pft[0:3, 0:128]), r=[pfk], w=["pct"])
        op("act", lambda e: e.copy(XCB_[:, 0:T], XC_[:, 0:T]), r=[kxc], w=[kxcb])
        op("dve", lambda e: e.tensor_scalar(GX2_[:, 0:T], GX2_[:, 0:T], 0.044715, 1.0, ALU.mult, ALU.add), r=[kgx2], w=[kgx2])
        yield
        op(pool_or_dve(ti), lambda e: e.tensor_tensor(GX2_[:, 0:T], GX2_[:, 0:T], GX_[:, 0:T], ALU.mult), r=[kgx2, kgx], w=[kgx2])
        yield

    def rg_back(ti, c, is_sample):
        T, R, nsub = tile_geom(is_sample)
        last_prompt = (not is_sample) and ti == 3
        S_ = c % 2
        if is_sample:
            kxr_, kxc, ktr, kti, kaa, kgx, kgx2, kxcb = [n + f"s{S_}" for n in RGN]
            XR_, XC_, TR_, TI_, AA_, GX_, GX2_ = [SRG[:, S_, i, :] for i in range(7)]
            XCB_ = SXCB[:, S_, :]
            RSQ_, rsqk = RSQS[:, S_, :], f"rsqs{S_}"
            REC_, reck = RECS[:, c, :], "recs"
            pcol = 32 + c
        else:
            kxr_, kxc, ktr, kti, kaa, kgx, kgx2, kxcb = KSET[S_]
            XR_, XC_, TR_, TI_, AA_, GX_, GX2_, XCB_ = XR[S_], XC[S_], TR[S_], TI[S_], AA[S_], GX[S_], GX2[S_], XCB[S_]
            RSQ_, rsqk = RSQB[S_], f"rsq{S_}"
            REC_, reck = REC[:, c, :], "rec"
            pcol = None
        pr, kr = pf()
        op("pe", lambda e: e.matmul(pr[:, 0:T], BDA[:, c, :], XCB_[:, 0:T], start=True, stop=True), r=["bda", kxcb], w=[kr])
        pi, ki = pf()
        op("pe", lambda e: e.matmul(pi[:, 0:T], BDX[:, c, :], XCB_[:, 0:T], start=True, stop=True), r=["bdx", kxcb], w=[ki])
        op("act", lambda e: e.activation(TR_[:, 0:T], pr[:, 0:T], AF.Tanh, bias=DER[:, c:c + 1], scale=0.5), r=[kr, "der"], w=[ktr])
        op("act", lambda e: e.activation(TI_[:, 0:T], pi[:, 0:T], AF.Tanh, bias=DER[:, 4 + c:5 + c], scale=0.5), r=[ki, "der"], w=[kti])
        yield
        op("act", lambda e: e.activation(GX2_[:, 0:T], GX2_[:, 0:T], AF.Tanh, scale=0.7978845608028654), r=[kgx2], w=[kgx2])
        yield
        op("act", lambda e: e.activation(AA_[:, 0:T], TR_[:, 0:T], AF.Exp, bias=DER[:, 8 + c:9 + c], scale=DER[:, 8 + c:9 + c]), r=[ktr, "der"], w=[kaa])
        yield
        op("act", lambda e: e.activation(TR_[:, 0:T], TR_[:, 0:T], AF.Exp, bias=DER[:, 12 + c:13 + c], scale=DER[:, 12 + c:13 + c]), r=[ktr, "der"], w=[ktr])
        yield
        op("act", lambda e: e.activation(TR_[:, 0:T], TR_[:, 0:T], AF.Sqrt, bias=QUARTC, scale=-0.25), r=[ktr, "der"], w=[ktr])
        yield
        op("dve", lambda e: e.scalar_tensor_tensor(TI_[:, 0:T], TI_[:, 0:T], 1.0, XC_[:, 0:T], ALU.add, ALU.mult), r=[kti, kxc], w=[kti])
        yield
        op(pool_or_dve(ti), lambda e: e.tensor_tensor(TI_[:, 0:T], TI_[:, 0:T], TR_[:, 0:T], ALU.mult), r=[kti, ktr], w=[kti])
        yield
        hk = f"hs{S_}"
        if not is_sample:
            op("dve", lambda e: e.tensor_tensor_scan(HSB[S_][:, 0:T], AA_[:, 0:T], TI_[:, 0:T], HCAR[:, c:c + 1], ALU.mult, ALU.add),
               r=[kaa, kti, "hcar"], w=[hk])
            yield
            op("dve", lambda e: e.tensor_copy(HCAR[:, c:c + 1], HSB[S_][:, T - 1:T]), r=[hk], w=["hcar"])
            if last_prompt:
                pft, pfk = pf()
                op("pe", lambda e: e.transpose(pft[0:1, 0:128], HSB[S_][:, T - 1:T], IDF[:, :]), r=[hk, "idf"], w=[pfk])
                op("act", lambda e: e.copy(PHT[0:1, c * 128:(c + 1) * 128], pft[0:1, 0:128]), r=[pfk], w=["pht"])
            hs_ap = HSB[S_][:, 0:T]
            hkeys = [hk]
        else:
            op("dve", lambda e: e.tensor_tensor(AA_[:, 0:T], AA_[:, 0:T], H0T[:, c, :], ALU.mult), r=[kaa, "h0t"], w=[kaa])
            op("dve", lambda e: e.tensor_tensor(AA_[:, 0:T], AA_[:, 0:T], TI_[:, 0:T], ALU.add), r=[kaa, kti], w=[kaa])
            hs_ap = AA_[:, 0:T]
            hkeys = [kaa]
            pft, pfk = pf()
            op("pe", lambda e: e.transpose(pft[0:NSMP, 0:128], hs_ap, IDF[:, :]), r=[kaa, "idf"], w=[pfk])
            op("act", lambda e: e.copy(ATOK[:, c * 128:(c + 1) * 128], pft[0:NSMP, 0:128]), r=[pfk], w=["atok"])
        yield
        op("dve", lambda e: e.scalar_tensor_tensor(GX_[:, 0:T], GX2_[:, 0:T], 1.0, GX_[:, 0:T], ALU.add, ALU.mult), r=[kgx, kgx2], w=[kgx])
        yield
        op(pool_or_dve(ti), lambda e: e.tensor_tensor(GX_[:, 0:T], GX_[:, 0:T], hs_ap, ALU.mult), r=[kgx] + hkeys, w=[kgx])
        yield
        op("act", lambda e: e.activation(RSQ_[:, 0:T], GX_[:, 0:T], AF.Square), r=[kgx], w=[rsqk])
        yield
        op("act", lambda e: e.activation(REC_[:, 0:T], GX_[:, 0:T], AF.Identity, scale=COLS[:, 32 + c:33 + c]), r=[kgx, "cols"], w=[reck])
        yield
        for s in range(nsub):
            col = pcol if is_sample else s * 4 + c
            op("pe", lambda e, s=s, col=col: e.matmul(PSTAT[0:R, col:col + 1], RSQ_[:, s * R:(s + 1) * R], ONES[:, 0:1], start=True, stop=True),
               r=[rsqk, "ones"], w=[PSTK], inc=(s == nsub - 1))
        yield

    def qkv(ti):
        T = TT
        last_prompt = ti == 3
        for c in range(4):
            pq, kq = win_chunk(CH["q"](c), T)
            op("act", lambda e, c=c, pq=pq: e.activation(QB[:, c, 0:T], pq[:, 0:T], AF.Identity, scale=0.125), r=[kq], w=["qb"])
            yield
        for g in range(2):
            pk_, kk = win_chunk(CH["kd"](g), T)
            op("dve", lambda e, g=g, pk_=pk_: e.tensor_copy(KT[0:64, g, 0, 128:128 + T], pk_[0:64, 0:T]), r=[kk], w=["kt"])
            op("dve", lambda e, g=g, pk_=pk_: e.tensor_copy(KT[64:128, g, 1, 128:128 + T], pk_[64:128, 0:T]), r=[kk], w=["kt"])
            if last_prompt:
                op("act", lambda e, g=g, pk_=pk_: e.copy(KF[g * 64:(g + 1) * 64, 0:T], pk_[g * 64:(g + 1) * 64, 0:T]), r=[kk], w=["kf"])
            yield
        pv_, kv = win_chunk(CH["v"](0), T)
        op("act", lambda e: e.copy(VF[:, 0:T], pv_[:, 0:T]), r=[kv], w=["vf"])
        yield
        pvt, pvk = pf()
        for i in range(4):
            op("pe", lambda e, i=i: e.transpose(pvt[:, i * 128:(i + 1) * 128], VF[:, i * 128:(i + 1) * 128], IDF[:, :]),
               r=["vf", "idf"], w=[pvk], inc=(i == 3))
        src = lambda: pvt[:, :].rearrange("p (i g d) -> p i g d", i=4, g=2)
        op("dve", lambda e: e.tensor_copy(VTOK[:, 1:5, :, 0, 0:64], src()), r=[pvk], w=["vtok"])
        op("dve", lambda e: e.tensor_copy(VTOK[:, 1:5, :, 1, 64:128], src()), r=[pvk], w=["vtok"])
        if last_prompt:
            op("act", lambda e: e.copy(KVOUT[:, 1, :], pvt[:, 384:512]), r=[pvk], w=["kvout1"])
            dma(lambda e: e.dma_start(out=pv_d, in_=KVOUT[:, 1, :]), r=["kvout1"])
            pkt, pkk = pf()
            op("pe", lambda e: e.transpose(pkt[:, 0:128], KF[:, 384:512], IDF[:, :]), r=["kf", "idf"], w=[pkk])
            op("act", lambda e: e.copy(KVOUT[:, 0, :], pkt[:, 0:128]), r=[pkk], w=["kvout0"])
            dma(lambda e: e.dma_start(out=pk_d, in_=KVOUT[:, 0, :]), r=["kvout0"])
        yield

    def attention(ti, chunks):
        T, R, nsub = TT, 128, 4
        steps = [(c, i) for c in chunks for i in range(4)]

        def scores(n):
            c, i = steps[n]
            g = c // 2
            first = (ti == 0 and i == 0)
            ps_, ksc = PFK[n % 2], f"pf{n % 2}"
            for hh in range(2):
                op("pe", lambda e, hh=hh: e.matmul(ps_[:, hh * 256:(hh + 1) * 256], QB[:, c, i * 128:(i + 1) * 128],
                                                   KT[:, g, hh, i * 128:i * 128 + 256], start=True, stop=False),
                   r=["qb", "kt"], w=[ksc], inc=False)
                op("pe", lambda e, hh=hh: e.matmul(ps_[:, hh * 256:(hh + 1) * 256], IDB[:, :], MASK[:, 1 if first else 0, 0:256],
                                                   start=False, stop=True),
                   r=["idb", "mask"], w=[ksc], inc=(hh == 1))

        def smcols(n):
            j = n % 2
            return j, 24 + 8 * j, f"sm{j}"

        def stage_a(n):
            c, i = steps[n]
            ps_, ksc = PFK[n % 2], f"pf{n % 2}"
            j, o, jk = smcols(n)
            op("dve", lambda e: e.tensor_reduce(SM[:, o:o + 1], ps_[:, 0:512], AX.X, ALU.max, negate=True), r=[ksc], w=[jk + "n"])
            op("dve", lambda e: e.tensor_scalar(SM[:, o:o + 1], SM[:, o:o + 1], NSINKB[:, 2 * c:2 * c + 1], NSINKB[:, 2 * c + 1:2 * c + 2], ALU.min, ALU.min),
               r=[jk + "n", "nsinkb"], w=[jk + "n"])

        def stage_b(n):
            c, i = steps[n]
            ps_, ksc = PFK[n % 2], f"pf{n % 2}"
            j, o, jk = smcols(n)
            for hh in range(2):
                op("act", lambda e, hh=hh: e.activation(P32[j][:, hh * 256:(hh + 1) * 256], ps_[:, hh * 256:(hh + 1) * 256], AF.Exp,
                                                        bias=SM[:, o:o + 1], scale=1.0, accum_out=SM[:, o + 1 + hh:o + 2 + hh]),
                   r=[ksc, jk + "n"], w=[f"p32_{j}", jk + f"s{hh}"])
            op("act", lambda e: e.activation(SM[:, o + 3:o + 5], SINKB[:, 2 * c:2 * c + 2], AF.Exp, bias=SM[:, o:o + 1], scale=1.0),
               r=["sinkb", jk + "n"], w=[jk + "e"])

        pbts = {}

        def stage_c1(n):
            c, i = steps[n]
            j, o, jk = smcols(n)
            op("dve", lambda e: e.tensor_tensor(SM[:, o + 5:o + 7], SM[:, o + 1:o + 3], SM[:, o + 3:o + 5], ALU.add), r=[jk + "s0", jk + "s1", jk + "e"], w=[jk + "d"])
            op("dve", lambda e: e.reciprocal(SM[:, o + 5:o + 7], SM[:, o + 5:o + 7]), r=[jk + "d"], w=[jk + "d"])
            for hh in range(2):
                op("dve", lambda e, hh=hh: e.tensor_scalar(PB16[j][:, hh * 256:(hh + 1) * 256], P32[j][:, hh * 256:(hh + 1) * 256],
                                                          SM[:, o + 5 + hh:o + 6 + hh], None, ALU.mult),
                   r=[f"p32_{j}", jk + "d"], w=[f"pb16_{j}"])
            pbt, pbk = pb()
            pbts[n] = (pbt, pbk)
            for q4 in range(4):
                op("pe", lambda e, q4=q4: e.transpose(pbt[:, q4 * 128:(q4 + 1) * 128], PB16[j][:, q4 * 128:(q4 + 1) * 128], IDB[:, :]),
                   r=[f"pb16_{j}", "idb"], w=[pbk], inc=(q4 == 3))

        def stage_c2(n):
            c, i = steps[n]
            g = c // 2
            j, o, jk = smcols(n)
            pbt, pbk = pbts.pop(n)
            op("act", lambda e: e.copy(PTS[j][:, :], pbt[:, 0:512]), r=[pbk], w=[f"pts{j}"])
            nn = 0
            for hh in range(2):
                for kb in range(2):
                    op("pe", lambda e, hh=hh, kb=kb, nn=nn: e.matmul(PO[:, i * 128:(i + 1) * 128], VTOK[:, i + kb, g, hh, :],
                                                                   PTS[j][:, (hh * 2 + kb) * 128:(hh * 2 + kb + 1) * 128],
                                                                   start=(nn == 0), stop=(nn == 3)),
                       r=["vtok", f"pts{j}"], w=[PKO], inc=(nn == 3))
                    nn += 1
            if i == 3:
                S_ = c % 2
                op("act", lambda e: e.activation(ASQ[S_][:, 0:T], PO[:, 0:T], AF.Square), r=[PKO], w=[f"asq{S_}"])
                op("act", lambda e: e.activation(ATT[:, c, 0:T], PO[:, 0:T], AF.Identity, scale=COLS[:, 36 + c:37 + c]), r=[PKO, "cols"], w=["att"])
                for s in range(nsub):
                    op("pe", lambda e, s=s: e.matmul(PSTAT[0:R, 16 + s * 4 + c:16 + s * 4 + c + 1], ASQ[S_][:, s * R:(s + 1) * R], ONES[:, 0:1], start=True, stop=True),
                       r=[f"asq{S_}", "ones"], w=[PSTK], inc=(s == nsub - 1))

        N = len(steps)
        scores(0)
        if N > 1:
            scores(1)
        stage_a(0)
        yield
        stage_b(0)
        yield
        if N > 1:
            stage_a(1)
            yield
        for n in range(N):
            stage_c1(n)
            yield
            if n + 1 < N:
                stage_b(n + 1)
            if n + 2 < N:
                scores(n + 2)
                stage_a(n + 2)
            yield
            stage_c2(n)
            yield

    def wout_residual(is_sample):
        T, R, nsub = tile_geom(is_sample)
        if is_sample:
            op("dve", lambda e: e.tensor_reduce(SM[0:R, 52:54], PSTAT[0:R, 32:40].rearrange("p (a c) -> p a c", c=4), AX.X, ALU.add), r=[PSTK], w=["gstat"])
            rstd_cols(SM[0:R, 16:17], SM[0:R, 52:53], ["gstat"], 1.0 / 512, EPS4C, R, 1, "rsm_r")
            rstd_cols(SM[0:R, 20:21], SM[0:R, 53:54], ["gstat"], 1.0 / 512, EPSC, R, 1, "rsm_a")
            recv = lambda c, s: RECS[:, c, :]
            attv = lambda c, s: ATTS[:, c, :]
            rk, ak = "recs", "atts"
        else:
            op("dve", lambda e: e.tensor_reduce(SM[0:R, 52:60], PSTAT[0:R, 0:32].rearrange("p (a c) -> p a c", c=4), AX.X, ALU.add), r=[PSTK], w=["gstat"])
            rstd_cols(SM[0:R, 16:16 + nsub], SM[0:R, 52:52 + nsub], ["gstat"], 1.0 / 512, EPS4C, R, nsub, "rsm_r")
            rstd_cols(SM[0:R, 20:20 + nsub], SM[0:R, 56:56 + nsub], ["gstat"], 1.0 / 512, EPSC, R, nsub, "rsm_a")
            recv = lambda c, s: REC[:, c, s * R:(s + 1) * R]
            attv = lambda c, s: ATT[:, c, s * R:(s + 1) * R]
            rk, ak = "rec", "att"
        for s in range(nsub):
            for dh in range(2):
                pa, ka = pf()
                for c in range(4):
                    op("pe", lambda e, s=s, dh=dh, c=c, pa=pa: e.matmul(pa[0:R, :], recv(c, s), WOUT[:, c, dh * 512:(dh + 1) * 512],
                                                                      start=(c == 0), stop=(c == 3)), r=[rk, "wout"], w=[ka], inc=(c == 3))
                pb2, kb2 = pf()
                for c in range(4):
                    op("pe", lambda e, s=s, dh=dh, c=c, pb2=pb2: e.matmul(pb2[0:R, :], attv(c, s), WOUT[:, 4 + c, dh * 512:(dh + 1) * 512],
                                                                        start=(c == 0), stop=(c == 3)), r=[ak, "wout"], w=[kb2], inc=(c == 3))
                rs_ = RS(is_sample, s)[:, dh * 512:(dh + 1) * 512]
                rk_ = RK(is_sample, s)
                op("dve", lambda e, s=s, pa=pa, rs_=rs_: e.scalar_tensor_tensor(rs_, pa[0:R, :], SM[0:R, 16 + s:17 + s], rs_, ALU.mult, ALU.add),
                   r=[ka, "rsm_r", rk_], w=[rk_])
                op("dve", lambda e, s=s, pb2=pb2, rs_=rs_: e.scalar_tensor_tensor(rs_, pb2[0:R, :], SM[0:R, 20 + s:21 + s], rs_, ALU.mult, ALU.add),
                   r=[kb2, "rsm_a", rk_], w=[rk_])

    def ffn(ti, rider, between=None, next_qkv=None, defer_final=False):
        T, R, nsub = TT, 128, 4
        ydst = yp_d[ti * TT:(ti + 1) * TT, :]
        norm_transpose(nsub, R, 1, 8, 12, "b")
        for f in range(NF):
            slot = rr["wgu"]; rr["wgu"] = (slot + 1) % NWGU
            dma(lambda e, f=f, slot=slot: e.dma_start(out=WGU[slot][:], in_=wgu_s[f]), r=[f"wgud{f}"], w=[f"wgu{slot}"])
            pgs = []
            for gu in range(2):
                pg, kg = pf(4)
                pgs.append((pg, kg))
                for k in range(8):
                    op("pe", lambda e, k=k, gu=gu, slot=slot, pg=pg: e.matmul(pg[:, 0:T], WGU[slot][:, gu, k, :], XT[:, k, 0:T], start=(k == 0), stop=(k == 7)),
                       r=[f"wgu{slot}", "xt"], w=[kg], inc=(k == 7))
                    if rider:
                        op("pe", lambda e, k=k, gu=gu, slot=slot: e.matmul(PSTAT[:, 64 + 32 * gu:64 + 32 * gu + NSMP], WGU[slot][:, gu, k, :], XTS[:, k, :],
                                                                         start=(k == 0), stop=(k == 7)),
                           r=[f"wgu{slot}", "xts"], w=[PSTK], inc=(k == 7))
            (pg, kg), (pu, ku) = pgs
            j = f % 2
            op("act", lambda e, j=j, pg=pg: e.activation(SIL[j][:, 0:T], pg[:, 0:T], AF.Silu), r=[kg], w=[f"sil{j}"])
            op("dve", lambda e, j=j, f=f, pu=pu: e.tensor_tensor(FFT[:, f * 512:f * 512 + T], SIL[j][:, 0:T], pu[:, 0:T], ALU.mult),
               r=[f"sil{j}", ku], w=[f"fft{f}"])
            if rider:
                op("act", lambda e: e.activation(SILS[:, :], PSTAT[:, 64:64 + NSMP], AF.Silu), r=[PSTK], w=["sils"])
                op("dve", lambda e, f=f: e.tensor_tensor(FFTS[:, f, :], SILS[:, :], PSTAT[:, 96:96 + NSMP], ALU.mult), r=["sils", PSTK], w=["ffts"])
        def pass_b():
            banks = (0, 1, 2, 3) if rider else (0, 1, 4, 5)
            for dh in range(2):
                accs = [(PFK[b], f"pf{b}") for b in banks]
                for f in range(NF):
                    slot = rr["wd"]; rr["wd"] = (slot + 1) % NWD
                    dma(lambda e, dh=dh, f=f, slot=slot: e.dma_start(out=WD[slot][:], in_=wd_s[dh, f]), r=[f"wdd{dh}_{f}"], w=[f"wd{slot}"])
                    for s in range(nsub):
                        op("pe", lambda e, s=s, f=f, slot=slot, a=accs[s][0]: e.matmul(a[0:R, :], FFT[:, f * 512 + s * R:f * 512 + (s + 1) * R], WD[slot][:, :],
                                                                                     start=(f == 0), stop=(f == NF - 1)),
                           r=[f"fft{f}", f"wd{slot}"], w=[accs[s][1]], inc=(f == NF - 1 or (s == nsub - 1 and not rider)))
                    if rider:
                        op("pe", lambda e, f=f, slot=slot: e.matmul(PO[0:NSMP, :], FFTS[:, f, :], WD[slot][:, :], start=(f == 0), stop=(f == NF - 1)),
                           r=["ffts", f"wd{slot}"], w=[PKO])
                    yield
                for s in range(nsub):
                    op("dve", lambda e, s=s, dh=dh, a=accs[s][0]: e.tensor_tensor(RES[:, s, dh * 512:(dh + 1) * 512], a[0:R, :], RES[:, s, dh * 512:(dh + 1) * 512], ALU.add),
                       r=[accs[s][1], f"res{s}"], w=[f"res{s}"])
                if rider:
                    op("dve", lambda e, dh=dh: e.tensor_tensor(XSMP[:, dh * 512:(dh + 1) * 512], PO[0:NSMP, :], XSMP[:, dh * 512:(dh + 1) * 512], ALU.add),
                       r=[PKO, "xsmp"], w=["xsmp"])
                yield

        side = []
        if between is not None:
            side.append(between)
        if (not rider) and next_qkv is not None:
            side.append(next_qkv)
        if side:
            mix(pass_b(), chain(*side), weights=[3, 1])
        else:
            run(pass_b())

        def final(nsub, R, smp, so, ro, tag, dst):
            for s in range(nsub):
                op("act", lambda e, s=s: e.activation(XN[s % 2][0:R, :], RS(smp, s), AF.Square, accum_out=SM[0:R, so + s:so + s + 1]),
                   r=[RK(smp, s)], w=[f"xn{s%2}", f"ss{tag}{s}"])
            rstd_cols(SM[0:R, ro:ro + nsub], SM[0:R, so:so + nsub], [f"ss{tag}{s}" for s in range(nsub)], 1.0 / D, EPSC, R, nsub, "rs" + tag)
            for s in range(nsub):
                op("dve", lambda e, s=s: e.scalar_tensor_tensor(RS(smp, s), RS(smp, s), SM[0:R, ro + s:ro + s + 1], GB[0:R, 2, :], ALU.mult, ALU.mult),
                   r=[RK(smp, s), "rs" + tag, "gb"], w=[RK(smp, s)])
                dma(lambda e, s=s: e.dma_start(out=dst[s * R:(s + 1) * R, :], in_=RS(smp, s)), r=[RK(smp, s)])

        if rider:
            final(1, NSMP, True, 48, 49, "d", ys_d)
        if not defer_final:
            final(nsub, R, False, 40, 44, "c", ydst)
            return None

        def final_later():
            for s in range(nsub):
                op("act", lambda e, s=s: e.activation(XN[s % 2][0:R, :], RS(False, s), AF.Square, accum_out=SM[0:R, 40 + s:41 + s]),
                   r=[RK(False, s)], w=[f"xn{s%2}", f"ssc{s}"])
                yield
            rstd_cols(SM[0:R, 44:44 + nsub], SM[0:R, 40:40 + nsub], [f"ssc{s}" for s in range(nsub)], 1.0 / D, EPSC, R, nsub, "rsc")
            yield
            for s in range(nsub):
                op("dve", lambda e, s=s: e.scalar_tensor_tensor(RS(False, s), RS(False, s), SM[0:R, 44 + s:45 + s], GB[0:R, 2, :], ALU.mult, ALU.mult),
                   r=[RK(False, s), "rsc", "gb"], w=[RK(False, s)])
                dma(lambda e, s=s: e.dma_start(out=ydst[s * R:(s + 1) * R, :], in_=RS(False, s)), r=[RK(False, s)])
                yield
            residual_load(ti + 1)
            yield

        return final_later()

    def phase_keys_barrier(to_ffn):
        rgk = KSET[0] + KSET[1]
        ffk = [f"fft{f}" for f in range(NF)] + ["sil0", "sil1"]
        if to_ffn:
            op("pool", lambda e: e.memset(SM[:, 62:63], 0.0), r=rgk, w=ffk + rgk)
        else:
            op("pool", lambda e: e.memset(SM[:, 63:64], 0.0), r=ffk, w=rgk + ffk)

    def chain(*gens):
        for g in gens:
            yield from g

    def prompt_mixer_rest(ti, skip_qkv=False, tail=None):
        if not skip_qkv:
            run(qkv(ti))
        mix(attention(ti, (0, 1)), chain(rg_front(ti, 0, False), rg_back(ti, 0, False)), chain(rg_front(ti, 1, False), rg_back(ti, 1, False)),
            tail, weights=([2, 2, 2, 1] if tail is not None else [1, 1, 1]))
        mix(attention(ti, (2, 3)), chain(rg_front(ti, 2, False), rg_back(ti, 2, False)), chain(rg_front(ti, 3, False), rg_back(ti, 3, False)),
            weights=[1, 1, 1])
        op("pool", lambda e: e.tensor_copy(KT[:, :, :, 0:128], KT[:, :, :, TT:TT + 128]), r=["kt"], w=["kt"])
        op("pool", lambda e: e.tensor_copy(VTOK[:, 0], VTOK[:, 4]), r=["vtok"], w=["vtok"])
        if ti == 3:
            dma(lambda e: e.dma_start(out=pc_d, in_=PCH[0:3, :]), r=["pct"])
            dma(lambda e: e.dma_start(out=ph_d.rearrange("(o c) -> o c", o=1), in_=PHT[0:1, :]), r=["pht"])
        wout_residual(False)

    def prompt_ffn(ti, rider):
        phase_keys_barrier(True)
        tail = ffn(ti, rider=rider, between=prefetch_norm1(ti + 1) if ti < 3 else None,
                   next_qkv=qkv(ti + 1) if (ti < 3 and not rider) else None, defer_final=(ti < 3))
        phase_keys_barrier(False)
        return tail

    def sample_z():
        for (grp, a, b, dst) in ((0, 0, 512, 0), (1, 1024, 1536, 512), (2, 1536, 1920, 1024)):
            pft, pfk = pf()
            for k in range(8):
                op("pe", lambda e, k=k, a=a, b=b, pft=pft: e.matmul(pft[0:NSMP, 0:b - a], XT[:, k, 0:NSMP], WINX[:, k, a:b], start=(k == 0), stop=(k == 7)),
                   r=["xt", "winx"], w=[pfk], inc=(k == 7))
            if grp == 1:
                op("act", lambda e, pft=pft: e.activation(STOK[:, 512:1024], pft[0:NSMP, 0:512], AF.Identity, scale=0.125), r=[pfk], w=STK)
            else:
                op("act", lambda e, pft=pft, a=a, b=b, dst=dst: e.copy(STOK[:, dst:dst + (b - a)], pft[0:NSMP, 0:b - a]), r=[pfk], w=STK)
        dma(lambda e: e.dma_start(out=sco_d[:, 2, :], in_=STOK[:, 0:512]), r=STK)
        dma(lambda e: e.dma_start(out=sk_d[:, 127, 0:64], in_=STOK[:, 1024:1088]), r=STK)
        dma(lambda e: e.dma_start(out=sk_d[:, 127, 64:128], in_=STOK[:, 1152:1216]), r=STK)
        dma(lambda e: e.dma_start(out=sv_d[:, 127, :], in_=STOK[:, 1280:1408]), r=STK)
        dma(lambda e: e.dma_start(out=q_s, in_=STOK[:, 512:1024]), r=STK, w=["q_s"])
        dma(lambda e: e.dma_start(out=kv_s[:, 0:64], in_=STOK[:, 1024:1088]), r=STK, w=["kv_s"])
        dma(lambda e: e.dma_start(out=kv_s[:, 64:128], in_=STOK[:, 1152:1216]), r=STK, w=["kv_s"])
        dma(lambda e: e.dma_start(out=kv_s[:, 128:256], in_=STOK[:, 1280:1408]), r=STK, w=["kv_s"])
        for h in range(8):
            g = h // 4
            dma(lambda e, h=h: e.dma_start(out=QS[h * 16:(h + 1) * 16, :], in_=q_s[:, h * 64:(h + 1) * 64]), r=["q_s"], w=["qs"])
            dma(lambda e, h=h, g=g: e.dma_start(out=KNEW[h * 16:(h + 1) * 16, :], in_=kv_s[:, g * 64:(g + 1) * 64]), r=["kv_s"], w=["knew"])
            dma(lambda e, h=h, g=g: e.dma_start(out=VNEW[h * 16:(h + 1) * 16, :], in_=kv_s[:, 128 + g * 64:128 + (g + 1) * 64]), r=["kv_s"], w=["vnew"])

    KV4 = KVB.rearrange("p (k g d) -> p k g d", g=2, d=64)

    def replicate(blk, which=0):
        r = which * 2 + blk // 8
        lb = blk % 8
        rep, rk = REPB[r // 2], f"pb16_{r // 2}"
        pft, pfk = pf()
        for g in range(2):
            op("pe", lambda e, g=g, pft=pft: e.matmul(pft[:, :].rearrange("p (k d) -> p k d", d=64), rep[:, (r % 2) * 2 + g, :], KV4[:, lb * 8:(lb + 1) * 8, g, :],
                                                      start=(g == 0), stop=(g == 1)), r=[rk] + KVK, w=[pfk], inc=(g == 1))
        return pft, pfk

    def sample_scores():
        for blk in range(16):
            pft, pfk = replicate(blk)
            tmp, tk = P32[blk % 2], f"p32_{blk % 2}"
            t3 = tmp[:, :].rearrange("p (k d) -> p k d", d=64)
            op("dve", lambda e, pft=pft, t3=t3: e.tensor_tensor(t3, pft[:, :].rearrange("p (k d) -> p k d", d=64),
                                                               QS[:, :].unsqueeze(1).broadcast_to([128, 8, 64]), ALU.mult), r=[pfk, "qs"], w=[tk])
            op("dve", lambda e, blk=blk, t3=t3: e.tensor_reduce(SCS[:, blk * 8:(blk + 1) * 8], t3, AX.X, ALU.add), r=[tk], w=["scs"])
        op("dve", lambda e: e.tensor_tensor(OPART[:, :], KNEW[:, :], QS[:, :], ALU.mult), r=["knew", "qs"], w=["opart"])
        op("dve", lambda e: e.tensor_reduce(SCS[:, 128:129], OPART[:, :], AX.X, ALU.add), r=["opart"], w=["scs"])
        op("dve", lambda e: e.tensor_reduce(SM[:, 48:49], SCS[:, 0:129], AX.X, ALU.max, negate=True), r=["scs"], w=["snmx"])
        op("dve", lambda e: e.tensor_scalar(SM[:, 48:49], SM[:, 48:49], NSINKB[:, 8:9], None, ALU.min), r=["snmx", "nsinkb"], w=["snmx"])
        op("act", lambda e: e.activation(PS_[:, 0:129], SCS[:, 0:129], AF.Exp, bias=SM[:, 48:49], scale=1.0, accum_out=SM[:, 49:50]), r=["scs", "snmx"], w=["ps_", "ssum"])
        op("act", lambda e: e.activation(SM[:, 50:51], SINKC[:, 0:1], AF.Exp, bias=SM[:, 48:49], scale=1.0), r=["sinkc", "snmx"], w=["ses"])
        op("dve", lambda e: e.tensor_tensor(SM[:, 51:52], SM[:, 49:50], SM[:, 50:51], ALU.add), r=["ssum", "ses"], w=["sden"])
        op("dve", lambda e: e.reciprocal(SM[:, 51:52], SM[:, 51:52]), r=["sden"], w=["sden"])
        op("dve", lambda e: e.tensor_scalar(PS_[:, 0:129], PS_[:, 0:129], SM[:, 51:52], None, ALU.mult), r=["ps_", "sden"], w=["ps_"])

    def sample_pv():
        op("dve", lambda e: e.tensor_scalar(OS[:, :], VNEW[:, :], PS_[:, 128:129], None, ALU.mult), r=["vnew", "ps_"], w=["os"])
        for blk in range(16):
            pft, pfk = replicate(blk, 1)
            tmp, tk = P32[blk % 2], f"p32_{blk % 2}"
            t3 = tmp[:, :].rearrange("p (k d) -> p k d", d=64)
            op("dve", lambda e, pft=pft, t3=t3, blk=blk: e.tensor_tensor(t3, pft[:, :].rearrange("p (k d) -> p k d", d=64),
                                                                        PS_[:, blk * 8:(blk + 1) * 8].unsqueeze(2).broadcast_to([128, 8, 64]), ALU.mult),
               r=[pfk, "ps_"], w=[tk])
            op("dve", lambda e, tmp=tmp: e.tensor_reduce(OPART[:, :], tmp[:, :].rearrange("p (k d) -> p d k", d=64), AX.X, ALU.add), r=[tk], w=["opart"])
            op("dve", lambda e: e.tensor_tensor(OS[:, :], OS[:, :], OPART[:, :], ALU.add), r=["os", "opart"], w=["os"])
        for h in range(8):
            dma(lambda e, h=h: e.dma_start(out=o_s[:, h * 64:(h + 1) * 64], in_=OS[h * 16:(h + 1) * 16, :]), r=["os"], w=["o_s"])
        dma(lambda e: e.dma_start(out=ATOK[:, :], in_=o_s), r=["o_s"], w=["atok"])

    def sample_tail():
        for c in range(4):
            po, ko = pf()
            op("pe", lambda e, c=c, po=po: e.transpose(po[:, 0:NSMP], ATOK[:, c * 128:(c + 1) * 128], IDF[0:NSMP, 0:NSMP]), r=["atok", "idf"], w=[ko])
            S_ = c % 2
            op("act", lambda e, S_=S_, po=po: e.activation(ASQS[:, S_, :], po[:, 0:NSMP], AF.Square), r=[ko], w=[f"asqs{S_}"])
            op("act", lambda e, c=c, po=po: e.activation(ATTS[:, c, :], po[:, 0:NSMP], AF.Identity, scale=COLS[:, 36 + c:37 + c]), r=[ko, "cols"], w=["atts"])
            op("pe", lambda e, S_=S_, c=c: e.matmul(PSTAT[0:NSMP, 36 + c:36 + c + 1], ASQS[:, S_, :], ONES[:, 0:1], start=True, stop=True),
               r=[f"asqs{S_}", "ones"], w=[PSTK])
        wout_residual(True)
        norm_transpose(1, NSMP, 1, 8, 12, "b", smp=True, xdst=XTS)

    try:
        _gate(1)
        load_norm1(4, True, do_load=False)
        sample_state()
        for c in range(4):
            run(rg_front(4, c, True))
            run(rg_back(4, c, True))
        dma(lambda e: e.dma_start(out=sho_d, in_=ATOK[:, :]), r=["atok"])
        sample_z()
        run(prefetch_norm1(0))
        stage_wout()
        sample_scores()
        run(qkv(0))
        sample_pv()
        residual_load(0)
        prompt_mixer_rest(0, skip_qkv=True)
        sample_tail()
        tail = prompt_ffn(0, True)
        dma(lambda e: e.dma_start(out=sco_d[:, 0:2, :], in_=sc_d[:, 512:1536].rearrange("b (j c) -> b j c", j=2)))
        dma(lambda e: e.dma_start(out=sk_d[:, 0:127, :], in_=ck_d[:, 1:128, :]))
        dma(lambda e: e.dma_start(out=sv_d[:, 0:127, :], in_=cv_d[:, 1:128, :]))
        for ti in range(1, 4):
            prompt_mixer_rest(ti, skip_qkv=(ti >= 2), tail=tail)
            tail = prompt_ffn(ti, False)
    except _Stop:
        pass
    P.finish()
    P.emit()
    st.close()
    return nc


_CACHE = {}


def _host_consts():
    ident = np.eye(128, dtype=np.float32)
    i = np.arange(128)[:, None]
    j = np.arange(128)[None, :]
    prev = np.where(j >= i, 0.0, NEG).astype(np.float32)
    own = np.where(j <= i, 0.0, NEG).astype(np.float32)
    m0 = np.concatenate([prev, own], 1)
    m1 = np.concatenate([np.full((128, 128), NEG, np.float32), own], 1)
    mask = np.stack([np.concatenate([m0, m0], 1), np.concatenate([m1, m1], 1)], 1)
    rep = np.zeros((128, 8, 128), np.float32)
    for r in range(4):
        for h in range(8):
            for b in range(NSMP):
                rep[r * NSMP + b, r * 2 + h // 4, h * 16 + b] = 1.0
    return ident, np.ascontiguousarray(mask), rep


def kernel(x_prompt, x_sample, cache_k_win, cache_v_win, state_conv, state_h,
           norm1_g, w_in, conv_w, conv_b, gate_a_w, gate_a_b, gate_x_w, gate_x_b,
           lru_lambda, attn_sinks, rec_norm_g, attn_norm_g, w_out, norm2_g,
           w_gate, w_up, w_down, final_norm_g):
    f32 = lambda a: np.ascontiguousarray(np.asarray(a, dtype=np.float32))
    if "nc" not in _CACHE:
        _CACHE["nc"] = build_program()
    nc = _CACHE["nc"]
    ident, mask, rep = _host_consts()
    cols = np.zeros((128, 48), np.float32)
    cw = f32(conv_w)[0]
    for c in range(4):
        for j in range(4):
            cols[:, 4 * c + j] = cw[j, c * 128:(c + 1) * 128]
    def colset(off, vec):
        v = f32(vec).reshape(-1)
        for c in range(4):
            cols[:, off + c] = v[c * 128:(c + 1) * 128]
    colset(16, conv_b); colset(20, gate_a_b); colset(24, gate_x_b); colset(28, lru_lambda)
    colset(32, rec_norm_g); colset(36, attn_norm_g)
    gbc = np.ascontiguousarray(np.stack([np.broadcast_to(f32(g).reshape(1, D), (128, D)) for g in (norm1_g, norm2_g, final_norm_g)], 0))
    sinks = f32(attn_sinks).reshape(8)
    sinkb = np.ascontiguousarray(np.broadcast_to(sinks[None, :], (128, 8)))
    sinkc = np.ascontiguousarray(np.repeat(sinks, 16).reshape(128, 1))
    shared = {
        "w_in": f32(w_in)[0], "w_out": f32(w_out)[0], "w_gate": f32(w_gate)[0], "w_up": f32(w_up)[0], "w_down": f32(w_down)[0],
        "gate_a_w": f32(gate_a_w)[0], "gate_x_w": f32(gate_x_w)[0], "cols": cols, "gbc": gbc, "sinkb": sinkb, "sinkc": sinkc,
        "ident": ident, "mask": mask, "rep": rep,
    }
    xp = f32(x_prompt); xs = f32(x_sample)[:, 0, :]
    ck = f32(cache_k_win)[0].reshape(128, 128, 128); cv = f32(cache_v_win)[0].reshape(128, 128, 128)
    sc = f32(state_conv)[0].reshape(128, 1536); sh = f32(state_h)[0]
    in_maps = []
    for c in range(NCORES):
        sl = slice(c * NSMP, (c + 1) * NSMP)
        m = dict(shared)
        m.update({"xp": xp[c], "xs": np.ascontiguousarray(xs[sl]), "ck": np.ascontiguousarray(ck[sl]), "cv": np.ascontiguousarray(cv[sl]),
                  "sc": np.ascontiguousarray(sc[sl]), "sh": np.ascontiguousarray(sh[sl])})
        in_maps.append(m)
    res = run_bass_kernel_spmd(nc, in_maps, core_ids=list(range(NCORES)))
    R = res.results
    cat = lambda k: np.concatenate([np.asarray(r[k]) for r in R], 0)
    y_p = np.stack([np.asarray(r["y_p"]) for r in R], 0)
    y_s = cat("y_s").reshape(128, 1, D)
    p_k = np.stack([np.asarray(r["p_k"]) for r in R], 0).reshape(1, 8, 128, 2, 64)
    p_v = np.stack([np.asarray(r["p_v"]) for r in R], 0).reshape(1, 8, 128, 2, 64)
    p_c = np.stack([np.asarray(r["p_conv"]) for r in R], 0).reshape(1, 8, 3, 512)
    p_h = np.stack([np.asarray(r["p_h"]) for r in R], 0).reshape(1, 8, 512)
    s_k = cat("s_k").reshape(1, 128, 128, 2, 64)
    s_v = cat("s_v").reshape(1, 128, 128, 2, 64)
    s_c = cat("s_conv").reshape(1, 128, 3, 512)
    s_h = cat("s_h").reshape(1, 128, 512)
    out = (y_p, y_s, p_k, p_v, p_c, p_h, s_k, s_v, s_c, s_h)
    return tuple(np.ascontiguousarray(o, dtype=np.float32) for o in out)
```

```python
from contextlib import ExitStack
import numpy as np
import concourse.bass as bass
import concourse.mybir as mybir
from concourse.bass_utils import run_bass_kernel_spmd

F32 = mybir.dt.float32
BF16 = mybir.dt.bfloat16
AF = mybir.ActivationFunctionType
ALU = mybir.AluOpType
AX = mybir.AxisListType

NCORES = 8
D = 1024
SEQ = 2048
NSMP = 16
DFF = 2816
NF = 22
EPS = 1e-6
TT = 512
NEG = -30000.0
STAGE = 99
NDUMMY = 0
SKIP = ''


class _Stop(Exception):
    pass


def _gate(n):
    if STAGE < n:
        raise _Stop()


class Prog:
    ENGS = (("pe", "tensor"), ("act", "scalar"), ("dve", "vector"), ("pool", "gpsimd"), ("sp", "sync"))

    def __init__(self, nc, ndma=24):
        self.nc = nc
        self.ops = {e: [] for e, _ in self.ENGS}
        self.cnt = {e: 0 for e, _ in self.ENGS}
        self.lastw = {}
        self.readers = {}
        self.waited = {e: {} for e, _ in self.ENGS}
        self.ndma = ndma
        self.dma_tot = [0] * ndma
        self.dma_rr = 0
        self.dma_rr2 = 0

    def _deps(self, eng, reads, writes):
        need = {}

        def add(tok):
            if tok is None:
                return
            k, v = tok
            if k == eng and eng == "pe":
                return
            if need.get(k, 0) < v:
                need[k] = v

        for b in reads:
            add(self.lastw.get(b))
        for b in writes:
            add(self.lastw.get(b))
            for t in self.readers.get(b, {}).items():
                add(t)
        waits = []
        for k, v in need.items():
            if isinstance(k, str):
                assert v <= self.cnt[k], f"wait on future token {k} {v} > {self.cnt[k]} (open PE group?)"
            if self.waited[eng].get(k, 0) >= v:
                continue
            self.waited[eng][k] = v
            waits.append((k, v))
        return waits

    def _reg(self, tok, reads, writes):
        k, v = tok
        for b in reads:
            d = self.readers.setdefault(b, {})
            if d.get(k, 0) < v:
                d[k] = v
        for b in writes:
            self.lastw[b] = tok
            self.readers[b] = {}

    def op(self, eng, fn, r=(), w=(), inc=True):
        r = list(r); w = list(w)
        for b in r:
            if isinstance(b, str) and (b.startswith("pf") or b.startswith("pb")) and b[2:].isdigit() and b not in w:
                w.append(b)
        waits = self._deps(eng, r, w)
        if inc:
            self.cnt[eng] += 1
            tok = (eng, self.cnt[eng])
        else:
            tok = (eng, self.cnt[eng] + 1)
        self._reg(tok, r, w)
        self.ops[eng].append((fn, waits, (eng, 1) if inc else None))

    def dma(self, fn, r=(), w=(), q="sp"):
        waits = self._deps(q, r, w)
        nsp = self.ndma - 16
        if q == "sp":
            j = self.dma_rr % nsp
            self.dma_rr += 1
        else:
            j = nsp + self.dma_rr2 % 16
            self.dma_rr2 += 1
        prev = self.dma_tot[j]
        key = ("dma", j)
        if prev > 0 and self.waited[q].get(key, 0) < prev:
            waits.append((key, prev))
            self.waited[q][key] = prev
        self.dma_tot[j] += 16
        tok = (key, self.dma_tot[j])
        self._reg(tok, r, w)
        self.ops[q].append((fn, waits, (key, 16)))

    def finish(self, q="sp"):
        waits = []
        for j in range(self.ndma):
            if self.dma_tot[j] > 0:
                waits.append((("dma", j), self.dma_tot[j]))
        self.ops[q].append((None, waits, None))

    def emit(self):
        nc = self.nc
        with ExitStack() as st:
            sems = {}
            for e, _ in self.ENGS:
                sems[e] = st.enter_context(nc.semaphore("s_" + e))
            for j in range(self.ndma):
                sems[("dma", j)] = st.enter_context(nc.semaphore(f"s_dma{j}"))
            block = st.enter_context(nc.Block())
            for e, attr in self.ENGS:
                ops = self.ops[e]

                def body(eng, ops=ops):
                    for fn, waits, inc in ops:
                        for k, v in waits:
                            eng.wait_ge(sems[k], v)
                        if fn is None:
                            continue
                        ins = fn(eng)
                        if inc is not None:
                            ins.then_inc(sems[inc[0]], inc[1])

                getattr(block, attr)(body)


def build_program():
    nc = bass.Bass("TRN2", target_bir_lowering=False)
    P = Prog(nc, ndma=56)

    def din(name, shape, dt=F32):
        return nc.dram_tensor(name, list(shape), dt, kind="ExternalInput").ap()

    def dout(name, shape):
        return nc.dram_tensor(name, list(shape), F32, kind="ExternalOutput").ap()

    xp_d = din("xp", [SEQ, D]); xs_d = din("xs", [NSMP, D])
    ck_d = din("ck", [NSMP, 128, 128]); cv_d = din("cv", [NSMP, 128, 128])
    sc_d = din("sc", [NSMP, 1536]); sh_d = din("sh", [NSMP, 512])
    win_d = din("w_in", [D, 1792]); wout_d = din("w_out", [D, D])
    wg_d = din("w_gate", [D, DFF]); wu_d = din("w_up", [D, DFF]); wdn_d = din("w_down", [DFF, D])
    ga_d = din("gate_a_w", [8, 64, 64]); gx_d = din("gate_x_w", [8, 64, 64])
    cols_d = din("cols", [128, 48])
    gb_d = din("gbc", [3, 128, D])
    sinkb_d = din("sinkb", [128, 8])
    sinkc_d = din("sinkc", [128, 1])
    ident_d = din("ident", [128, 128])
    rep_d = din("rep", [128, 8, 128])
    mask_d = din("mask", [128, 2, 512])

    yp_d = dout("y_p", [SEQ, D]); ys_d = dout("y_s", [NSMP, D])
    pk_d = dout("p_k", [128, 128]); pv_d = dout("p_v", [128, 128])
    pc_d = dout("p_conv", [3, 512]); ph_d = dout("p_h", [512])
    sk_d = dout("s_k", [NSMP, 128, 128]); sv_d = dout("s_v", [NSMP, 128, 128])
    sco_d = dout("s_conv", [NSMP, 3, 512]); sho_d = dout("s_h", [NSMP, 512])

    wgu_s = nc.dram_tensor("wgu_scr", [NF, 128, 2, 8, 128], BF16, kind="Internal").ap()
    wd_s = nc.dram_tensor("wd_scr", [2, NF, 128, 512], BF16, kind="Internal").ap()
    q_s = nc.dram_tensor("q_scr", [NSMP, 512], F32, kind="Internal").ap()
    kv_s = nc.dram_tensor("kv_scr", [NSMP, 384], F32, kind="Internal").ap()
    o_s = nc.dram_tensor("o_scr", [NSMP, 512], F32, kind="Internal").ap()

    st = ExitStack()

    def sb(name, shape, dt=F32):
        return st.enter_context(nc.sbuf_tensor(name, list(shape), dt))

    def psum(name, shape, dt=F32):
        return st.enter_context(nc.psum_tensor(name, list(shape), dt))

    WINX = sb("WINX", [128, 8, 1920], BF16)
    WOUT = sb("WOUT", [128, 8, D], BF16)
    NWGU, NWD = 3, 3
    WGU = [sb(f"WGU{i}", [128, 2, 8, 128], BF16) for i in range(NWGU)]
    WD = [sb(f"WD{i}", [128, 512], BF16) for i in range(NWD)]
    BDA = sb("BDA", [128, 4, 128], BF16); BDX = sb("BDX", [128, 4, 128], BF16)
    IDF = sb("IDF", [128, 128]); IDB = sb("IDB", [128, 128], BF16); ONES = sb("ONES", [128, 2], BF16)
    MASK = sb("MASK", [128, 2, 512], BF16)
    GB = sb("GB", [128, 3, D])
    COLS = sb("COLS", [128, 48]); DER = sb("DER", [128, 32])
    SINKB = sb("SINKB", [128, 8]); SINKC = sb("SINKC", [128, 1]); NSINKB = sb("NSINKB", [128, 9])
    RES = sb("RES", [128, 4, D])
    XN = [sb(f"XN{i}", [128, D], BF16) for i in range(2)]
    XT = sb("XT", [128, 8, TT], BF16)
    SCR = sb("SCR", [128, 17408], BF16)

    def carve32(off_kb, ncols):
        a = int(off_kb * 512)
        return SCR[:, a:a + 2 * ncols].bitcast(F32)

    XR = [carve32(0, 515), carve32(16, 515)]
    XC = [carve32(2.25, 512), carve32(18.25, 512)]
    TR = [carve32(4.25, 512), carve32(20.25, 512)]
    TI = [carve32(6.25, 512), carve32(22.25, 512)]
    AA = [carve32(8.25, 512), carve32(24.25, 512)]
    GX = [carve32(10.25, 512), carve32(26.25, 512)]
    GX2 = [carve32(12.25, 512), carve32(28.25, 512)]
    XCB = [SCR[:, int(14.25 * 512):int(14.25 * 512) + 512], SCR[:, int(30.25 * 512):int(30.25 * 512) + 512]]
    RSQ = [SCR[:, int(15.25 * 512):int(15.25 * 512) + 384], SCR[:, int(31.25 * 512):int(31.25 * 512) + 384]]
    FFT = SCR[:, 0:NF * 512]
    SIL = [carve32(22, 512), carve32(24, 512)]
    STOK = SCR[0:NSMP, 8192:12288].bitcast(F32)
    RGN = ["xr", "xc", "tr", "ti", "aa", "gx", "gx2", "xcb"]
    KSET = [[n + "0" for n in RGN], [n + "1" for n in RGN]]
    STK = ["stok"] + KSET[1]

    HSB = [sb(f"HS{i}", [128, TT]) for i in range(2)]
    HCAR = sb("HCAR", [128, 4])
    RSQB = [sb(f"RSQB{i}", [128, TT], BF16) for i in range(2)]
    HALO = sb("HALO", [128, 4, 3])
    QB = sb("QB", [128, 4, TT], BF16)
    KT = sb("KT", [128, 2, 2, 128 + TT], BF16)
    KF = sb("KF", [128, TT]); VF = sb("VF", [128, TT])
    VTOK = sb("VTOK", [128, 5, 2, 2, 128], BF16)
    KVOUT = sb("KVOUT", [128, 2, 128])
    REC = sb("REC", [128, 4, TT], BF16); ATT = sb("ATT", [128, 4, TT], BF16)
    ASQ = [sb(f"ASQ{i}", [128, TT], BF16) for i in range(2)]
    P32 = [sb(f"P32_{i}", [128, 512]) for i in range(2)]
    BDS = P32[0][:, :].rearrange("p (c j) -> p c j", c=4)
    PB16 = [sb(f"PB16_{i}", [128, 512], BF16) for i in range(2)]
    PTS = [sb(f"PTS{i}", [128, 512], BF16) for i in range(2)]
    SM = sb("SM", [128, 64])
    SCT = sb("SCT", [128, 12, NSMP]); H0T = sb("H0T", [128, 4, NSMP])
    XSTG = sb("XSTG", [128, 2, D])
    QS = sb("QS", [128, 64]); KNEW = sb("KNEW", [128, 64]); VNEW = sb("VNEW", [128, 64])
    SCS = sb("SCS", [128, 132]); PS_ = sb("PS_", [128, 132]); OS = sb("OS", [128, 64]); OPART = sb("OPART", [128, 64])
    ATOK = sb("ATOK", [NSMP, 512])
    PCH = sb("PCH", [3, 512]); PHT = sb("PHT", [1, 512])
    SRG = sb("SRG", [128, 2, 7, 24]); SXCB = sb("SXCB", [128, 2, NSMP], BF16)
    RECS = sb("RECS", [128, 4, NSMP], BF16); ATTS = sb("ATTS", [128, 4, NSMP], BF16)
    RSQS = sb("RSQS", [128, 2, NSMP], BF16); ASQS = sb("ASQS", [128, 2, NSMP], BF16)
    XSMP = sb("XSMP", [NSMP, D])
    XTS = sb("XTS", [128, 8, NSMP], BF16)
    FFTS = sb("FFTS", [128, NF, NSMP], BF16); SILS = sb("SILS", [128, NSMP])

    PBK = [psum(f"PBK{i}", [128, 1024], BF16) for i in range(2)]
    PFK = [psum(f"PFK{i}", [128, 512]) for i in range(6)]
    rr = {"pf": 0, "pb": 0, "wgu": 0, "wd": 0}

    def pf(n=2):
        if n == 2:
            i = 2 + rr["pf"] % 2
        else:
            i = rr["pf"] % 4
        rr["pf"] += 1
        return PFK[i], f"pf{i}"

    def pb():
        i = rr["pb"]; rr["pb"] = (i + 1) % 2
        return PBK[i], f"pb{i}"

    op, dma = P.op, P.dma

    dma(lambda e: e.dma_start(out=COLS[:], in_=cols_d), w=["cols"])
    dma(lambda e: e.dma_start(out=IDF[:], in_=ident_d), w=["idf"])
    dma(lambda e: e.dma_start(out=MASK[:], in_=mask_d), w=["mask"], q="pool")
    dma(lambda e: e.dma_start(out=SINKB[:], in_=sinkb_d), w=["sinkb"])
    dma(lambda e: e.dma_start(out=SINKC[:], in_=sinkc_d), w=["sinkc"])
    dma(lambda e: e.dma_start(out=GB[:], in_=gb_d.rearrange("g p d -> p g d")), w=["gb"])
    dma(lambda e: e.dma_start(out=XSMP[:, :], in_=xs_d), w=["xsmp"])
    dma(lambda e: e.dma_start(out=STOK[:, 0:1536], in_=sc_d), w=STK)
    dma(lambda e: e.dma_start(out=STOK[:, 1536:2048], in_=sh_d), w=STK)
    KVB = RES[:, :, :].rearrange("p s d -> p (s d)").bitcast(BF16)
    KVK = ["res0", "res1", "res2", "res3"]
    REPB = [PB16[i][:, 0:512].rearrange("p (g m) -> p g m", g=4) for i in range(2)]
    dma(lambda e: e.dma_start(out=REPB[0], in_=rep_d[:, 0:4, :]), w=["pb16_0"], q="pool")
    dma(lambda e: e.dma_start(out=REPB[1], in_=rep_d[:, 4:8, :]), w=["pb16_1"], q="pool")
    op("pool", lambda e: e.memset(KVB, 0.0), w=KVK)
    WST = [WGU[i][:].rearrange("p a k j -> p (a k j)").bitcast(F32) for i in range(3)]
    stg_rr = [0]

    def stage_cast(eng, dst, src, sk_, wk):
        if eng == "dve":
            op("dve", lambda e: e.tensor_copy(dst, src), r=[sk_], w=[wk])
        else:
            op("act", lambda e: e.copy(dst, src), r=[sk_], w=[wk])

    def stage_win():
        for k in range(8):
            for half in range(2):
                i = stg_rr[0] % 3; stg_rr[0] += 1
                stg, sk_ = WST[i], f"wgu{i}"
                dma(lambda e, k=k, half=half, stg=stg: e.dma_start(out=stg[:, 0:896], in_=win_d[k * 128:(k + 1) * 128, half * 896:(half + 1) * 896]), w=[sk_])
                eng = "dve" if (2 * k + half) % 2 == 0 else "act"
                if half == 0:
                    stage_cast(eng, WINX[:, k, 0:896], stg[:, 0:896], sk_, "winx")
                else:
                    stage_cast(eng, WINX[:, k, 896:1536], stg[:, 0:640], sk_, "winx")
                    oth = "act" if eng == "dve" else "dve"
                    for (a, lo, n) in ((1536, 640, 64), (1600, 640, 64), (1664, 704, 64), (1728, 704, 64), (1792, 768, 128)):
                        stage_cast(oth, WINX[:, k, a:a + n], stg[:, lo:lo + n], sk_, "winx")

    def stage_wout():
        for k in range(8):
            i = stg_rr[0] % 3; stg_rr[0] += 1
            stg, sk_ = WST[i], f"wgu{i}"
            dma(lambda e, k=k, stg=stg: e.dma_start(out=stg[:, 0:D], in_=wout_d[k * 128:(k + 1) * 128, :]), w=[sk_])
            stage_cast("dve" if k % 2 == 0 else "act", WOUT[:, k, :], stg[:, 0:D], sk_, "wout")

    stage_win()
    for r, (src_d, k0) in enumerate(((ck_d, 0), (ck_d, 64), (cv_d, 0), (cv_d, 64))):
        dma(lambda e, r=r, src_d=src_d, k0=k0: e.dma_start(out=KVB[r * NSMP:(r + 1) * NSMP, :], in_=src_d[:, k0:k0 + 64, :].rearrange("b k e -> b (k e)")),
            r=["winx"], w=KVK, q="pool")
    op("dve", lambda e: e.memset(BDS, 0.0), w=["bds", "p32_0"])
    for (src, dst, nm) in ((ga_d, BDA, "bda"), (gx_d, BDX, "bdx")):
        for h in range(2):
            dma(lambda e, src=src, h=h: e.dma_start(out=BDS[h * 64:(h + 1) * 64, :, h * 64:(h + 1) * 64],
                                                    in_=src[h:8:2].rearrange("c i j -> i c j")), r=[], w=["bds"])
        op("dve", lambda e, dst=dst: e.tensor_copy(dst[:], BDS), r=["bds"], w=[nm, "p32_0"])
    op("dve", lambda e: e.tensor_copy(IDB[:], IDF[:]), r=["idf"], w=["idb"])
    op("dve", lambda e: e.memset(ONES[:], 1.0), w=["ones"])
    op("dve", lambda e: e.tensor_scalar(NSINKB[:, 0:8], SINKB[:, :], -1.0, None, ALU.mult), r=["sinkb"], w=["nsinkb"])
    op("dve", lambda e: e.tensor_scalar(NSINKB[:, 8:9], SINKC[:, :], -1.0, None, ALU.mult), r=["sinkc", "nsinkb"], w=["nsinkb"])
    op("pool", lambda e: e.memset(KT[:], 0.0), w=["kt"])
    op("pool", lambda e: e.memset(VTOK[:], 0.0), w=["vtok"])
    op("pool", lambda e: e.memset(HALO[:], 0.0), w=["halo"])
    op("pool", lambda e: e.memset(HCAR[:], 0.0), w=["hcar"])
    op("dve", lambda e: e.tensor_scalar(DER[:, 0:8], COLS[:, 20:28], 0.5, None, ALU.mult), r=["cols"], w=["der"])
    op("act", lambda e: e.activation(DER[:, 20:24], COLS[:, 28:32], AF.Exp, scale=-1.0), r=["cols", "der"], w=["der"])
    op("act", lambda e: e.activation(DER[:, 24:28], DER[:, 20:24], AF.Ln, bias=1.0, scale=1.0), r=["der"], w=["der"])
    op("dve", lambda e: e.tensor_scalar(DER[:, 8:12], DER[:, 24:28], -4.0, None, ALU.mult), r=["der"], w=["der"])
    op("dve", lambda e: e.tensor_scalar(DER[:, 12:16], DER[:, 24:28], -8.0, None, ALU.mult), r=["der"], w=["der"])
    op("dve", lambda e: e.memset(DER[:, 16:17], EPS), r=["der"], w=["der"])
    op("dve", lambda e: e.memset(DER[:, 17:18], 4.0 * EPS), r=["der"], w=["der"])
    op("dve", lambda e: e.memset(DER[:, 18:19], 0.25), r=["der"], w=["der"])
    op("dve", lambda e: e.memset(DER[:, 19:20], 0.0), r=["der"], w=["der"])
    EPSC, EPS4C, QUARTC = DER[:, 16:17], DER[:, 17:18], DER[:, 18:19]
    for _i in range(SKIP.count('e')):
        op("act", lambda e: e.copy(SM[:, 60:61], DER[:, 19:20]), r=["der"], w=["dummy"])
    for _i in range(SKIP.count('f')):
        op("dve", lambda e: e.tensor_copy(SM[:, 61:62], DER[:, 19:20]), r=["der"], w=["dummy2"])
    wgv = wg_d.rearrange("(k p) (f j) -> f p k j", p=128, j=128)
    wuv = wu_d.rearrange("(k p) (f j) -> f p k j", p=128, j=128)
    wdv = wdn_d.rearrange("(f p) (h j) -> h f p j", p=128, j=512)
    for f in range(NF):
        dma(lambda e, f=f: e.dma_start(out=wgu_s[f, :, 0], in_=wgv[f]), r=(["winx"] if f == 0 else []), w=[f"wgud{f}"], q="pool")
        dma(lambda e, f=f: e.dma_start(out=wgu_s[f, :, 1], in_=wuv[f]), w=[f"wgud{f}"], q="pool")
    for h in range(2):
        for f0 in (0, 11):
            dma(lambda e, h=h, f0=f0: e.dma_start(out=wd_s[h, f0:f0 + 11], in_=wdv[h, f0:f0 + 11]),
                w=[f"wdd{h}_{f}" for f in range(f0, f0 + 11)], q="pool")

    def RS(smp, s):
        return XSMP[0:NSMP, :] if smp else RES[:, s, :]

    def RK(smp, s):
        return "xsmp" if smp else f"res{s}"

    def rstd_cols(dst_cols, src_ap, src_keys, scale, eps_col, R, n, tag):
        op("act", lambda e: e.activation(dst_cols, src_ap, AF.Sqrt, bias=eps_col[0:R], scale=scale), r=src_keys + ["der"], w=[tag])
        op("dve", lambda e: e.reciprocal(dst_cols, dst_cols), r=[tag], w=[tag])

    def norm_transpose(nsub, R, gidx, ss_off, rs_off, tag, smp=False, xdst=None):
        xdst = XT if xdst is None else xdst
        xk = "xt" if xdst is XT else "xts"
        for s in range(nsub):
            op("act", lambda e, s=s: e.activation(XN[s % 2][0:R, :], RS(smp, s), AF.Square, accum_out=SM[0:R, ss_off + s:ss_off + s + 1]),
               r=[RK(smp, s)], w=[f"xn{s%2}", f"ss{tag}{s}"])
        rstd_cols(SM[0:R, rs_off:rs_off + nsub], SM[0:R, ss_off:ss_off + nsub], [f"ss{tag}{s}" for s in range(nsub)],
                  1.0 / D, EPSC, R, nsub, f"rs{tag}")
        for s in range(nsub):
            op("dve", lambda e, s=s: e.scalar_tensor_tensor(XN[s % 2][0:R, :], RS(smp, s), SM[0:R, rs_off + s:rs_off + s + 1],
                                                            GB[0:R, gidx, :], ALU.mult, ALU.mult),
               r=[RK(smp, s), f"rs{tag}", "gb"], w=[f"xn{s%2}"])
            pbt, pbk = pb()
            for k in range(8):
                op("pe", lambda e, s=s, k=k, pbt=pbt: e.transpose(pbt[:, k * 128:k * 128 + R], XN[s % 2][0:R, k * 128:(k + 1) * 128], IDB[0:R, 0:R]),
                   r=[f"xn{s%2}", "idb"], w=[pbk], inc=(k == 7))
            op("act", lambda e, s=s, pbt=pbt: e.copy(xdst[:, :, s * R:(s + 1) * R], pbt[:, :].rearrange("p (k t) -> p k t", k=8)[:, :, 0:R]),
               r=[pbk], w=[xk])

    CH = {"xr": lambda c: c, "gr": lambda c: 4 + c, "q": lambda c: 8 + c, "kd": lambda g: 12 + g, "v": lambda _: 14}

    def win_chunk(ch, T):
        pft, pfk = pf()
        for k in range(8):
            op("pe", lambda e, k=k, pft=pft: e.matmul(pft[:, 0:T], WINX[:, k, ch * 128:(ch + 1) * 128], XT[:, k, 0:T],
                                                       start=(k == 0), stop=(k == 7)),
               r=["winx", "xt"], w=[pfk], inc=(k == 7))
        return pft, pfk

    def run(gen):
        for _ in gen:
            pass

    def mix(*gens, weights=None):
        gens = [g for g in gens if g is not None]
        wts = dict(zip(gens, weights or [1] * len(gens)))
        while gens:
            for g in list(gens):
                for _ in range(wts[g]):
                    try:
                        next(g)
                    except StopIteration:
                        gens.remove(g)
                        break

    PSTAT, PSTK = PFK[5], "pf5"
    PO, PKO = PFK[4], "pf4"

    def tile_geom(is_sample):
        return (NSMP, NSMP, 1) if is_sample else (TT, 128, 4)

    def load_norm1(ti, is_sample, do_load=True):
        T, R, nsub = tile_geom(is_sample)
        xsrc = xs_d if is_sample else xp_d[ti * TT:(ti + 1) * TT, :]
        if do_load:
            for s in range(nsub):
                dma(lambda e, s=s: e.dma_start(out=RS(is_sample, s), in_=xsrc[s * R:(s + 1) * R, :]), w=[RK(is_sample, s)])
        norm_transpose(nsub, R, 0, 0, 4, "a", smp=is_sample)

    def prefetch_norm1(ti):
        R = 128
        xsrc = xp_d[ti * TT:(ti + 1) * TT, :]
        for pair in range(2):
            for q in range(2):
                s = pair * 2 + q
                dma(lambda e, s=s, q=q: e.dma_start(out=XSTG[:, q, :], in_=xsrc[s * R:(s + 1) * R, :]), w=[f"xstg{q}"])
                op("act", lambda e, s=s, q=q: e.activation(XN[q][:, :], XSTG[:, q, :], AF.Square, accum_out=SM[:, s:s + 1]),
                   r=[f"xstg{q}"], w=[f"xn{q}", f"ssa{s}"])
            rstd_cols(SM[:, 4 + 2 * pair:6 + 2 * pair], SM[:, 2 * pair:2 * pair + 2], [f"ssa{2 * pair}", f"ssa{2 * pair + 1}"], 1.0 / D, EPSC, R, 2, "rsa")
            for q in range(2):
                s = pair * 2 + q
                op("dve", lambda e, s=s, q=q: e.scalar_tensor_tensor(XN[q][:, :], XSTG[:, q, :], SM[:, 4 + s:5 + s], GB[:, 0, :], ALU.mult, ALU.mult),
                   r=[f"xstg{q}", "rsa", "gb"], w=[f"xn{q}"])
                pbt, pbk = pb()
                for k in range(8):
                    op("pe", lambda e, q=q, k=k, pbt=pbt: e.transpose(pbt[:, k * 128:(k + 1) * 128], XN[q][:, k * 128:(k + 1) * 128], IDB[:, :]),
                       r=[f"xn{q}", "idb"], w=[pbk], inc=(k == 7))
                op("act", lambda e, s=s, pbt=pbt: e.copy(XT[:, :, s * R:(s + 1) * R], pbt[:, :].rearrange("p (k t) -> p k t", k=8)), r=[pbk], w=["xt"])
                yield

    def residual_load(ti):
        xsrc = xp_d[ti * TT:(ti + 1) * TT, :]
        for s in range(4):
            dma(lambda e, s=s: e.dma_start(out=RES[:, s, :], in_=xsrc[s * 128:(s + 1) * 128, :]), w=[f"res{s}"])

    def sample_state():
        for half in range(2):
            pft, pfk = pf()
            for i in range(8):
                op("pe", lambda e, i=i, half=half, pft=pft: e.transpose(pft[:, i * NSMP:(i + 1) * NSMP],
                                                                       STOK[:, (half * 8 + i) * 128:(half * 8 + i + 1) * 128], IDF[0:NSMP, 0:NSMP]),
                   r=STK + ["idf"], w=[pfk], inc=(i == 7))
            if half == 0:
                op("act", lambda e, pft=pft: e.copy(SCT[:, 0:8, :], pft[:, 0:8 * NSMP].rearrange("p (a b) -> p a b", b=NSMP)), r=[pfk], w=["sct"])
            else:
                op("act", lambda e, pft=pft: e.copy(SCT[:, 8:12, :], pft[:, 0:4 * NSMP].rearrange("p (a b) -> p a b", b=NSMP)), r=[pfk], w=["sct"])
                op("act", lambda e, pft=pft: e.copy(H0T[:, :, :], pft[:, 4 * NSMP:8 * NSMP].rearrange("p (a b) -> p a b", b=NSMP)), r=[pfk], w=["h0t"])

    def pool_or_dve(ti):
        return "dve"

    def rg_front(ti, c, is_sample):
        T, R, nsub = tile_geom(is_sample)
        last_prompt = (not is_sample) and ti == 3
        S_ = c % 2
        if is_sample:
            kxr_, kxc, ktr, kti, kaa, kgx, kgx2, kxcb = [n + f"s{S_}" for n in RGN]
            XR_, XC_, TR_, TI_, AA_, GX_, GX2_ = [SRG[:, S_, i, :] for i in range(7)]
            XCB_ = SXCB[:, S_, :]
            RSQ_, rsqk = RSQS[:, S_, :], f"rsqs{S_}"
            REC_, reck = RECS[:, c, :], "recs"
            pcol = 32 + c
        else:
            kxr_, kxc, ktr, kti, kaa, kgx, kgx2, kxcb = KSET[S_]
            XR_, XC_, TR_, TI_, AA_, GX_, GX2_, XCB_ = XR[S_], XC[S_], TR[S_], TI[S_], AA[S_], GX[S_], GX2[S_], XCB[S_]
            RSQ_, rsqk = RSQB[S_], f"rsq{S_}"
            REC_, reck = REC[:, c, :], "rec"
            pcol = None
        pxr, kxr = win_chunk(CH["xr"](c), T)
        op("act", lambda e: e.copy(XR_[:, 3:3 + T], pxr[:, 0:T]), r=[kxr], w=[kxr_])
        yield
        pgr, kgr = win_chunk(CH["gr"](c), T)
        op("act", lambda e: e.copy(GX_[:, 0:T], pgr[:, 0:T]), r=[kgr], w=[kgx])
        op("act", lambda e: e.activation(GX2_[:, 0:T], pgr[:, 0:T], AF.Square), r=[kgr], w=[kgx2])
        yield
        cw = lambda j: COLS[:, 4 * c + j:4 * c + j + 1]
        if not is_sample:
            op("dve", lambda e: e.tensor_copy(XR_[:, 0:3], HALO[:, c, :]), r=["halo"], w=[kxr_])
            srcs = [XR_[:, j:j + T] for j in range(4)]
            skeys = [kxr_]
        else:
            srcs = [SCT[:, j * 4 + c, :] for j in range(3)] + [XR_[:, 3:3 + T]]
            skeys = [kxr_, "sct"]
        op("dve", lambda e: e.tensor_scalar(XC_[:, 0:T], srcs[0], cw(0), COLS[:, 16 + c:17 + c], ALU.mult, ALU.add),
           r=skeys + ["cols"], w=[kxc])
        yield
        for j in range(1, 4):
            op("dve", lambda e, j=j: e.scalar_tensor_tensor(XC_[:, 0:T], srcs[j], cw(j), XC_[:, 0:T], ALU.mult, ALU.add),
               r=skeys + ["cols", kxc], w=[kxc])
            yield
        if not is_sample:
            op("dve", lambda e: e.tensor_copy(HALO[:, c, :], XR_[:, T:T + 3]), r=[kxr_], w=["halo"])
            if last_prompt:
                pft, pfk = pf()
                op("pe", lambda e: e.transpose(pft[0:3, 0:128], XR_[:, T:T + 3], IDF[:, :]), r=[kxr_, "idf"], w=[pfk])
                op("act", lambda e: e.copy(PCH[0:3, c * 128:(c + 1) * 128], pft[0:3, 0:128]), r=[pfk], w=["pct"])
        op("act", lambda e: e.copy(XCB_[:, 0:T], XC_[:, 0:T]), r=[kxc], w=[kxcb])
        op("dve", lambda e: e.tensor_scalar(GX2_[:, 0:T], GX2_[:, 0:T], 0.044715, 1.0, ALU.mult, ALU.add), r=[kgx2], w=[kgx2])
        yield
        op(pool_or_dve(ti), lambda e: e.tensor_tensor(GX2_[:, 0:T], GX2_[:, 0:T], GX_[:, 0:T], ALU.mult), r=[kgx2, kgx], w=[kgx2])
        yield

    def rg_back(ti, c, is_sample):
        T, R, nsub = tile_geom(is_sample)
        last_prompt = (not is_sample) and ti == 3
        S_ = c % 2
        if is_sample:
            kxr_, kxc, ktr, kti, kaa, kgx, kgx2, kxcb = [n + f"s{S_}" for n in RGN]
            XR_, XC_, TR_, TI_, AA_, GX_, GX2_ = [SRG[:, S_, i, :] for i in range(7)]
            XCB_ = SXCB[:, S_, :]
            RSQ_, rsqk = RSQS[:, S_, :], f"rsqs{S_}"
            REC_, reck = RECS[:, c, :], "recs"
            pcol = 32 + c
        else:
            kxr_, kxc, ktr, kti, kaa, kgx, kgx2, kxcb = KSET[S_]
            XR_, XC_, TR_, TI_, AA_, GX_, GX2_, XCB_ = XR[S_], XC[S_], TR[S_], TI[S_], AA[S_], GX[S_], GX2[S_], XCB[S_]
            RSQ_, rsqk = RSQB[S_], f"rsq{S_}"
            REC_, reck = REC[:, c, :], "rec"
            pcol = None
        pr, kr = pf()
        op("pe", lambda e: e.matmul(pr[:, 0:T], BDA[:, c, :], XCB_[:, 0:T], start=True, stop=True), r=["bda", kxcb], w=[kr])
        pi, ki = pf()
        op("pe", lambda e: e.matmul(pi[:, 0:T], BDX[:, c, :], XCB_[:, 0:T], start=True, stop=True), r=["bdx", kxcb], w=[ki])
        op("act", lambda e: e.activation(TR_[:, 0:T], pr[:, 0:T], AF.Tanh, bias=DER[:, c:c + 1], scale=0.5), r=[kr, "der"], w=[ktr])
        op("act", lambda e: e.activation(TI_[:, 0:T], pi[:, 0:T], AF.Tanh, bias=DER[:, 4 + c:5 + c], scale=0.5), r=[ki, "der"], w=[kti])
        yield
        op("act", lambda e: e.activation(GX2_[:, 0:T], GX2_[:, 0:T], AF.Tanh, scale=0.7978845608028654), r=[kgx2], w=[kgx2])
        yield
        op("act", lambda e: e.activation(AA_[:, 0:T], TR_[:, 0:T], AF.Exp, bias=DER[:, 8 + c:9 + c], scale=DER[:, 8 + c:9 + c]), r=[ktr, "der"], w=[kaa])
        yield
        op("act", lambda e: e.activation(TR_[:, 0:T], TR_[:, 0:T], AF.Exp, bias=DER[:, 12 + c:13 + c], scale=DER[:, 12 + c:13 + c]), r=[ktr, "der"], w=[ktr])
        yield
        op("act", lambda e: e.activation(TR_[:, 0:T], TR_[:, 0:T], AF.Sqrt, bias=QUARTC, scale=-0.25), r=[ktr, "der"], w=[ktr])
        yield
        op("dve", lambda e: e.scalar_tensor_tensor(TI_[:, 0:T], TI_[:, 0:T], 1.0, XC_[:, 0:T], ALU.add, ALU.mult), r=[kti, kxc], w=[kti])
        yield
        op(pool_or_dve(ti), lambda e: e.tensor_tensor(TI_[:, 0:T], TI_[:, 0:T], TR_[:, 0:T], ALU.mult), r=[kti, ktr], w=[kti])
        yield
        hk = f"hs{S_}"
        if not is_sample:
            op("dve", lambda e: e.tensor_tensor_scan(HSB[S_][:, 0:T], AA_[:, 0:T], TI_[:, 0:T], HCAR[:, c:c + 1], ALU.mult, ALU.add),
               r=[kaa, kti, "hcar"], w=[hk])
            yield
            op("dve", lambda e: e.tensor_copy(HCAR[:, c:c + 1], HSB[S_][:, T - 1:T]), r=[hk], w=["hcar"])
            if last_prompt:
                pft, pfk = pf()
                op("pe", lambda e: e.transpose(pft[0:1, 0:128], HSB[S_][:, T - 1:T], IDF[:, :]), r=[hk, "idf"], w=[pfk])
                op("act", lambda e: e.copy(PHT[0:1, c * 128:(c + 1) * 128], pft[0:1, 0:128]), r=[pfk], w=["pht"])
            hs_ap = HSB[S_][:, 0:T]
            hkeys = [hk]
        else:
            op("dve", lambda e: e.tensor_tensor(AA_[:, 0:T], AA_[:, 0:T], H0T[:, c, :], ALU.mult), r=[kaa, "h0t"], w=[kaa])
            op("dve", lambda e: e.tensor_tensor(AA_[:, 0:T], AA_[:, 0:T], TI_[:, 0:T], ALU.add), r=[kaa, kti], w=[kaa])
            hs_ap = AA_[:, 0:T]
            hkeys = [kaa]
            pft, pfk = pf()
            op("pe", lambda e: e.transpose(pft[0:NSMP, 0:128], hs_ap, IDF[:, :]), r=[kaa, "idf"], w=[pfk])
            op("act", lambda e: e.copy(ATOK[:, c * 128:(c + 1) * 128], pft[0:NSMP, 0:128]), r=[pfk], w=["atok"])
        yield
        op("dve", lambda e: e.scalar_tensor_tensor(GX_[:, 0:T], GX2_[:, 0:T], 1.0, GX_[:, 0:T], ALU.add, ALU.mult), r=[kgx, kgx2], w=[kgx])
        yield
        op(pool_or_dve(ti), lambda e: e.tensor_tensor(GX_[:, 0:T], GX_[:, 0:T], hs_ap, ALU.mult), r=[kgx] + hkeys, w=[kgx])
        yield
        op("act", lambda e: e.activation(RSQ_[:, 0:T], GX_[:, 0:T], AF.Square), r=[kgx], w=[rsqk])
        yield
        op("act", lambda e: e.activation(REC_[:, 0:T], GX_[:, 0:T], AF.Identity, scale=COLS[:, 32 + c:33 + c]), r=[kgx, "cols"], w=[reck])
        yield
        for s in range(nsub):
            col = pcol if is_sample else s * 4 + c
            op("pe", lambda e, s=s, col=col: e.matmul(PSTAT[0:R, col:col + 1], RSQ_[:, s * R:(s + 1) * R], ONES[:, 0:1], start=True, stop=True),
               r=[rsqk, "ones"], w=[PSTK], inc=(s == nsub - 1))
        yield

    def qkv(ti):
        T = TT
        last_prompt = ti == 3
        for c in range(4):
            pq, kq = win_chunk(CH["q"](c), T)
            op("act", lambda e, c=c, pq=pq: e.activation(QB[:, c, 0:T], pq[:, 0:T], AF.Identity, scale=0.125), r=[kq], w=["qb"])
            yield
        for g in range(2):
            pk_, kk = win_chunk(CH["kd"](g), T)
            op("dve", lambda e, g=g, pk_=pk_: e.tensor_copy(KT[0:64, g, 0, 128:128 + T], pk_[0:64, 0:T]), r=[kk], w=["kt"])
            op("dve", lambda e, g=g, pk_=pk_: e.tensor_copy(KT[64:128, g, 1, 128:128 + T], pk_[64:128, 0:T]), r=[kk], w=["kt"])
            if last_prompt:
                op("act", lambda e, g=g, pk_=pk_: e.copy(KF[g * 64:(g + 1) * 64, 0:T], pk_[g * 64:(g + 1) * 64, 0:T]), r=[kk], w=["kf"])
            yield
        pv_, kv = win_chunk(CH["v"](0), T)
        op("act", lambda e: e.copy(VF[:, 0:T], pv_[:, 0:T]), r=[kv], w=["vf"])
        yield
        pvt, pvk = pf()
        for i in range(4):
            op("pe", lambda e, i=i: e.transpose(pvt[:, i * 128:(i + 1) * 128], VF[:, i * 128:(i + 1) * 128], IDF[:, :]),
               r=["vf", "idf"], w=[pvk], inc=(i == 3))
        src = lambda: pvt[:, :].rearrange("p (i g d) -> p i g d", i=4, g=2)
        op("dve", lambda e: e.tensor_copy(VTOK[:, 1:5, :, 0, 0:64], src()), r=[pvk], w=["vtok"])
        op("dve", lambda e: e.tensor_copy(VTOK[:, 1:5, :, 1, 64:128], src()), r=[pvk], w=["vtok"])
        if last_prompt:
            op("act", lambda e: e.copy(KVOUT[:, 1, :], pvt[:, 384:512]), r=[pvk], w=["kvout1"])
            dma(lambda e: e.dma_start(out=pv_d, in_=KVOUT[:, 1, :]), r=["kvout1"])
            pkt, pkk = pf()
            op("pe", lambda e: e.transpose(pkt[:, 0:128], KF[:, 384:512], IDF[:, :]), r=["kf", "idf"], w=[pkk])
            op("act", lambda e: e.copy(KVOUT[:, 0, :], pkt[:, 0:128]), r=[pkk], w=["kvout0"])
            dma(lambda e: e.dma_start(out=pk_d, in_=KVOUT[:, 0, :]), r=["kvout0"])
        yield

    def attention(ti, chunks):
        T, R, nsub = TT, 128, 4
        steps = [(c, i) for c in chunks for i in range(4)]

        def scores(n):
            c, i = steps[n]
            g = c // 2
            first = (ti == 0 and i == 0)
            ps_, ksc = PFK[n % 2], f"pf{n % 2}"
            for hh in range(2):
                op("pe", lambda e, hh=hh: e.matmul(ps_[:, hh * 256:(hh + 1) * 256], QB[:, c, i * 128:(i + 1) * 128],
                                                   KT[:, g, hh, i * 128:i * 128 + 256], start=True, stop=False),
                   r=["qb", "kt"], w=[ksc], inc=False)
                op("pe", lambda e, hh=hh: e.matmul(ps_[:, hh * 256:(hh + 1) * 256], IDB[:, :], MASK[:, 1 if first else 0, 0:256],
                                                   start=False, stop=True),
                   r=["idb", "mask"], w=[ksc], inc=(hh == 1))

        def smcols(n):
            j = n % 2
            return j, 24 + 8 * j, f"sm{j}"

        def stage_a(n):
            c, i = steps[n]
            ps_, ksc = PFK[n % 2], f"pf{n % 2}"
            j, o, jk = smcols(n)
            op("dve", lambda e: e.tensor_reduce(SM[:, o:o + 1], ps_[:, 0:512], AX.X, ALU.max, negate=True), r=[ksc], w=[jk + "n"])
            op("dve", lambda e: e.tensor_scalar(SM[:, o:o + 1], SM[:, o:o + 1], NSINKB[:, 2 * c:2 * c + 1], NSINKB[:, 2 * c + 1:2 * c + 2], ALU.min, ALU.min),
               r=[jk + "n", "nsinkb"], w=[jk + "n"])

        def stage_b(n):
            c, i = steps[n]
            ps_, ksc = PFK[n % 2], f"pf{n % 2}"
            j, o, jk = smcols(n)
            for hh in range(2):
                op("act", lambda e, hh=hh: e.activation(P32[j][:, hh * 256:(hh + 1) * 256], ps_[:, hh * 256:(hh + 1) * 256], AF.Exp,
                                                        bias=SM[:, o:o + 1], scale=1.0, accum_out=SM[:, o + 1 + hh:o + 2 + hh]),
                   r=[ksc, jk + "n"], w=[f"p32_{j}", jk + f"s{hh}"])
            op("act", lambda e: e.activation(SM[:, o + 3:o + 5], SINKB[:, 2 * c:2 * c + 2], AF.Exp, bias=SM[:, o:o + 1], scale=1.0),
               r=["sinkb", jk + "n"], w=[jk + "e"])

        pbts = {}

        def stage_c1(n):
            c, i = steps[n]
            j, o, jk = smcols(n)
            op("dve", lambda e: e.tensor_tensor(SM[:, o + 5:o + 7], SM[:, o + 1:o + 3], SM[:, o + 3:o + 5], ALU.add), r=[jk + "s0", jk + "s1", jk + "e"], w=[jk + "d"])
            op("dve", lambda e: e.reciprocal(SM[:, o + 5:o + 7], SM[:, o + 5:o + 7]), r=[jk + "d"], w=[jk + "d"])
            for hh in range(2):
                op("dve", lambda e, hh=hh: e.tensor_scalar(PB16[j][:, hh * 256:(hh + 1) * 256], P32[j][:, hh * 256:(hh + 1) * 256],
                                                          SM[:, o + 5 + hh:o + 6 + hh], None, ALU.mult),
                   r=[f"p32_{j}", jk + "d"], w=[f"pb16_{j}"])
            pbt, pbk = pb()
            pbts[n] = (pbt, pbk)
            for q4 in range(4):
                op("pe", lambda e, q4=q4: e.transpose(pbt[:, q4 * 128:(q4 + 1) * 128], PB16[j][:, q4 * 128:(q4 + 1) * 128], IDB[:, :]),
                   r=[f"pb16_{j}", "idb"], w=[pbk], inc=(q4 == 3))

        def stage_c2(n):
            c, i = steps[n]
            g = c // 2
            j, o, jk = smcols(n)
            pbt, pbk = pbts.pop(n)
            op("act", lambda e: e.copy(PTS[j][:, :], pbt[:, 0:512]), r=[pbk], w=[f"pts{j}"])
            nn = 0
            for hh in range(2):
                for kb in range(2):
                    op("pe", lambda e, hh=hh, kb=kb, nn=nn: e.matmul(PO[:, i * 128:(i + 1) * 128], VTOK[:, i + kb, g, hh, :],
                                                                   PTS[j][:, (hh * 2 + kb) * 128:(hh * 2 + kb + 1) * 128],
                                                                   start=(nn == 0), stop=(nn == 3)),
                       r=["vtok", f"pts{j}"], w=[PKO], inc=(nn == 3))
                    nn += 1
            if i == 3:
                S_ = c % 2
                op("act", lambda e: e.activation(ASQ[S_][:, 0:T], PO[:, 0:T], AF.Square), r=[PKO], w=[f"asq{S_}"])
                op("act", lambda e: e.activation(ATT[:, c, 0:T], PO[:, 0:T], AF.Identity, scale=COLS[:, 36 + c:37 + c]), r=[PKO, "cols"], w=["att"])
                for s in range(nsub):
                    op("pe", lambda e, s=s: e.matmul(PSTAT[0:R, 16 + s * 4 + c:16 + s * 4 + c + 1], ASQ[S_][:, s * R:(s + 1) * R], ONES[:, 0:1], start=True, stop=True),
                       r=[f"asq{S_}", "ones"], w=[PSTK], inc=(s == nsub - 1))

        N = len(steps)
        scores(0)
        if N > 1:
            scores(1)
        stage_a(0)
        yield
        stage_b(0)
        yield
        if N > 1:
            stage_a(1)
            yield
        for n in range(N):
            stage_c1(n)
            yield
            if n + 1 < N:
                stage_b(n + 1)
            if n + 2 < N:
                scores(n + 2)
                stage_a(n + 2)
            yield
            stage_c2(n)
            yield

    def wout_residual(is_sample):
        T, R, nsub = tile_geom(is_sample)
        if is_sample:
            op("dve", lambda e: e.tensor_reduce(SM[0:R, 52:54], PSTAT[0:R, 32:40].rearrange("p (a c) -> p a c", c=4), AX.X, ALU.add), r=[PSTK], w=["gstat"])
            rstd_cols(SM[0:R, 16:17], SM[0:R, 52:53], ["gstat"], 1.0 / 512, EPS4C, R, 1, "rsm_r")
            rstd_cols(SM[0:R, 20:21], SM[0:R, 53:54], ["gstat"], 1.0 / 512, EPSC, R, 1, "rsm_a")
            recv = lambda c, s: RECS[:, c, :]
            attv = lambda c, s: ATTS[:, c, :]
            rk, ak = "recs", "atts"
        else:
            op("dve", lambda e: e.tensor_reduce(SM[0:R, 52:60], PSTAT[0:R, 0:32].rearrange("p (a c) -> p a c", c=4), AX.X, ALU.add), r=[PSTK], w=["gstat"])
            rstd_cols(SM[0:R, 16:16 + nsub], SM[0:R, 52:52 + nsub], ["gstat"], 1.0 / 512, EPS4C, R, nsub, "rsm_r")
            rstd_cols(SM[0:R, 20:20 + nsub], SM[0:R, 56:56 + nsub], ["gstat"], 1.0 / 512, EPSC, R, nsub, "rsm_a")
            recv = lambda c, s: REC[:, c, s * R:(s + 1) * R]
            attv = lambda c, s: ATT[:, c, s * R:(s + 1) * R]
            rk, ak = "rec", "att"
        for s in range(nsub):
            for dh in range(2):
                pa, ka = pf()
                for c in range(4):
                    op("pe", lambda e, s=s, dh=dh, c=c, pa=pa: e.matmul(pa[0:R, :], recv(c, s), WOUT[:, c, dh * 512:(dh + 1) * 512],
                                                                      start=(c == 0), stop=(c == 3)), r=[rk, "wout"], w=[ka], inc=(c == 3))
                pb2, kb2 = pf()
                for c in range(4):
                    op("pe", lambda e, s=s, dh=dh, c=c, pb2=pb2: e.matmul(pb2[0:R, :], attv(c, s), WOUT[:, 4 + c, dh * 512:(dh + 1) * 512],
                                                                        start=(c == 0), stop=(c == 3)), r=[ak, "wout"], w=[kb2], inc=(c == 3))
                rs_ = RS(is_sample, s)[:, dh * 512:(dh + 1) * 512]
                rk_ = RK(is_sample, s)
                op("dve", lambda e, s=s, pa=pa, rs_=rs_: e.scalar_tensor_tensor(rs_, pa[0:R, :], SM[0:R, 16 + s:17 + s], rs_, ALU.mult, ALU.add),
                   r=[ka, "rsm_r", rk_], w=[rk_])
                op("dve", lambda e, s=s, pb2=pb2, rs_=rs_: e.scalar_tensor_tensor(rs_, pb2[0:R, :], SM[0:R, 20 + s:21 + s], rs_, ALU.mult, ALU.add),
                   r=[kb2, "rsm_a", rk_], w=[rk_])

    def ffn(ti, rider, between=None, next_qkv=None):
        T, R, nsub = TT, 128, 4
        ydst = yp_d[ti * TT:(ti + 1) * TT, :]
        norm_transpose(nsub, R, 1, 8, 12, "b")
        for f in range(NF):
            slot = rr["wgu"]; rr["wgu"] = (slot + 1) % NWGU
            dma(lambda e, f=f, slot=slot: e.dma_start(out=WGU[slot][:], in_=wgu_s[f]), r=[f"wgud{f}"], w=[f"wgu{slot}"])
            pgs = []
            for gu in range(2):
                pg, kg = pf(4)
                pgs.append((pg, kg))
                for k in range(8):
                    op("pe", lambda e, k=k, gu=gu, slot=slot, pg=pg: e.matmul(pg[:, 0:T], WGU[slot][:, gu, k, :], XT[:, k, 0:T], start=(k == 0), stop=(k == 7)),
                       r=[f"wgu{slot}", "xt"], w=[kg], inc=(k == 7))
                    if rider:
                        op("pe", lambda e, k=k, gu=gu, slot=slot: e.matmul(PSTAT[:, 64 + 32 * gu:64 + 32 * gu + NSMP], WGU[slot][:, gu, k, :], XTS[:, k, :],
                                                                         start=(k == 0), stop=(k == 7)),
                           r=[f"wgu{slot}", "xts"], w=[PSTK], inc=(k == 7))
            (pg, kg), (pu, ku) = pgs
            j = f % 2
            op("act", lambda e, j=j, pg=pg: e.activation(SIL[j][:, 0:T], pg[:, 0:T], AF.Silu), r=[kg], w=[f"sil{j}"])
            op("dve", lambda e, j=j, f=f, pu=pu: e.tensor_tensor(FFT[:, f * 512:f * 512 + T], SIL[j][:, 0:T], pu[:, 0:T], ALU.mult),
               r=[f"sil{j}", ku], w=[f"fft{f}"])
            if rider:
                op("act", lambda e: e.activation(SILS[:, :], PSTAT[:, 64:64 + NSMP], AF.Silu), r=[PSTK], w=["sils"])
                op("dve", lambda e, f=f: e.tensor_tensor(FFTS[:, f, :], SILS[:, :], PSTAT[:, 96:96 + NSMP], ALU.mult), r=["sils", PSTK], w=["ffts"])
        def pass_b():
            banks = (0, 1, 2, 3) if rider else (0, 1, 4, 5)
            for dh in range(2):
                accs = [(PFK[b], f"pf{b}") for b in banks]
                for f in range(NF):
                    slot = rr["wd"]; rr["wd"] = (slot + 1) % NWD
                    dma(lambda e, dh=dh, f=f, slot=slot: e.dma_start(out=WD[slot][:], in_=wd_s[dh, f]), r=[f"wdd{dh}_{f}"], w=[f"wd{slot}"])
                    for s in range(nsub):
                        op("pe", lambda e, s=s, f=f, slot=slot, a=accs[s][0]: e.matmul(a[0:R, :], FFT[:, f * 512 + s * R:f * 512 + (s + 1) * R], WD[slot][:, :],
                                                                                     start=(f == 0), stop=(f == NF - 1)),
                           r=[f"fft{f}", f"wd{slot}"], w=[accs[s][1]], inc=(f == NF - 1 or (s == nsub - 1 and not rider)))
                    if rider:
                        op("pe", lambda e, f=f, slot=slot: e.matmul(PO[0:NSMP, :], FFTS[:, f, :], WD[slot][:, :], start=(f == 0), stop=(f == NF - 1)),
                           r=["ffts", f"wd{slot}"], w=[PKO])
                    yield
                for s in range(nsub):
                    op("dve", lambda e, s=s, dh=dh, a=accs[s][0]: e.tensor_tensor(RES[:, s, dh * 512:(dh + 1) * 512], a[0:R, :], RES[:, s, dh * 512:(dh + 1) * 512], ALU.add),
                       r=[accs[s][1], f"res{s}"], w=[f"res{s}"])
                if rider:
                    op("dve", lambda e, dh=dh: e.tensor_tensor(XSMP[:, dh * 512:(dh + 1) * 512], PO[0:NSMP, :], XSMP[:, dh * 512:(dh + 1) * 512], ALU.add),
                       r=[PKO, "xsmp"], w=["xsmp"])
                yield

        side = []
        if between is not None:
            side.append(between)
        if (not rider) and next_qkv is not None:
            side.append(next_qkv)
        if side:
            mix(pass_b(), chain(*side), weights=[3, 1])
        else:
            run(pass_b())

        def final(nsub, R, smp, so, ro, tag, dst):
            for s in range(nsub):
                op("act", lambda e, s=s: e.activation(XN[s % 2][0:R, :], RS(smp, s), AF.Square, accum_out=SM[0:R, so + s:so + s + 1]),
                   r=[RK(smp, s)], w=[f"xn{s%2}", f"ss{tag}{s}"])
            rstd_cols(SM[0:R, ro:ro + nsub], SM[0:R, so:so + nsub], [f"ss{tag}{s}" for s in range(nsub)], 1.0 / D, EPSC, R, nsub, "rs" + tag)
            for s in range(nsub):
                op("dve", lambda e, s=s: e.scalar_tensor_tensor(RS(smp, s), RS(smp, s), SM[0:R, ro + s:ro + s + 1], GB[0:R, 2, :], ALU.mult, ALU.mult),
                   r=[RK(smp, s), "rs" + tag, "gb"], w=[RK(smp, s)])
                dma(lambda e, s=s: e.dma_start(out=dst[s * R:(s + 1) * R, :], in_=RS(smp, s)), r=[RK(smp, s)])

        final(nsub, R, False, 40, 44, "c", ydst)
        if rider:
            final(1, NSMP, True, 48, 49, "d", ys_d)

    def phase_keys_barrier(to_ffn):
        rgk = KSET[0] + KSET[1]
        ffk = [f"fft{f}" for f in range(NF)] + ["sil0", "sil1"]
        if to_ffn:
            op("pool", lambda e: e.memset(SM[:, 62:63], 0.0), r=rgk, w=ffk + rgk)
        else:
            op("pool", lambda e: e.memset(SM[:, 63:64], 0.0), r=ffk, w=rgk + ffk)

    def chain(*gens):
        for g in gens:
            yield from g

    def prompt_mixer_rest(ti, skip_qkv=False):
        if not skip_qkv:
            run(qkv(ti))
        mix(attention(ti, (0, 1)), chain(rg_front(ti, 0, False), rg_back(ti, 0, False)), chain(rg_front(ti, 1, False), rg_back(ti, 1, False)),
            weights=[1, 1, 1])
        mix(attention(ti, (2, 3)), chain(rg_front(ti, 2, False), rg_back(ti, 2, False)), chain(rg_front(ti, 3, False), rg_back(ti, 3, False)),
            weights=[1, 1, 1])
        op("act", lambda e: e.copy(KT[:, :, :, 0:128], KT[:, :, :, TT:TT + 128]), r=["kt"], w=["kt"])
        op("act", lambda e: e.copy(VTOK[:, 0], VTOK[:, 4]), r=["vtok"], w=["vtok"])
        if ti == 3:
            dma(lambda e: e.dma_start(out=pc_d, in_=PCH[0:3, :]), r=["pct"])
            dma(lambda e: e.dma_start(out=ph_d.rearrange("(o c) -> o c", o=1), in_=PHT[0:1, :]), r=["pht"])
        wout_residual(False)

    def prompt_ffn(ti, rider):
        phase_keys_barrier(True)
        ffn(ti, rider=rider, between=prefetch_norm1(ti + 1) if ti < 3 else None,
            next_qkv=qkv(ti + 1) if (ti < 3 and not rider) else None)
        phase_keys_barrier(False)
        if ti < 3:
            residual_load(ti + 1)

    def sample_z():
        for (grp, a, b, dst) in ((0, 0, 512, 0), (1, 1024, 1536, 512), (2, 1536, 1920, 1024)):
            pft, pfk = pf()
            for k in range(8):
                op("pe", lambda e, k=k, a=a, b=b, pft=pft: e.matmul(pft[0:NSMP, 0:b - a], XT[:, k, 0:NSMP], WINX[:, k, a:b], start=(k == 0), stop=(k == 7)),
                   r=["xt", "winx"], w=[pfk], inc=(k == 7))
            if grp == 1:
                op("act", lambda e, pft=pft: e.activation(STOK[:, 512:1024], pft[0:NSMP, 0:512], AF.Identity, scale=0.125), r=[pfk], w=STK)
            else:
                op("act", lambda e, pft=pft, a=a, b=b, dst=dst: e.copy(STOK[:, dst:dst + (b - a)], pft[0:NSMP, 0:b - a]), r=[pfk], w=STK)
        dma(lambda e: e.dma_start(out=sco_d[:, 2, :], in_=STOK[:, 0:512]), r=STK)
        dma(lambda e: e.dma_start(out=sk_d[:, 127, 0:64], in_=STOK[:, 1024:1088]), r=STK)
        dma(lambda e: e.dma_start(out=sk_d[:, 127, 64:128], in_=STOK[:, 1152:1216]), r=STK)
        dma(lambda e: e.dma_start(out=sv_d[:, 127, :], in_=STOK[:, 1280:1408]), r=STK)
        dma(lambda e: e.dma_start(out=q_s, in_=STOK[:, 512:1024]), r=STK, w=["q_s"])
        dma(lambda e: e.dma_start(out=kv_s[:, 0:64], in_=STOK[:, 1024:1088]), r=STK, w=["kv_s"])
        dma(lambda e: e.dma_start(out=kv_s[:, 64:128], in_=STOK[:, 1152:1216]), r=STK, w=["kv_s"])
        dma(lambda e: e.dma_start(out=kv_s[:, 128:256], in_=STOK[:, 1280:1408]), r=STK, w=["kv_s"])
        for h in range(8):
            g = h // 4
            dma(lambda e, h=h: e.dma_start(out=QS[h * 16:(h + 1) * 16, :], in_=q_s[:, h * 64:(h + 1) * 64]), r=["q_s"], w=["qs"])
            dma(lambda e, h=h, g=g: e.dma_start(out=KNEW[h * 16:(h + 1) * 16, :], in_=kv_s[:, g * 64:(g + 1) * 64]), r=["kv_s"], w=["knew"])
            dma(lambda e, h=h, g=g: e.dma_start(out=VNEW[h * 16:(h + 1) * 16, :], in_=kv_s[:, 128 + g * 64:128 + (g + 1) * 64]), r=["kv_s"], w=["vnew"])

    KV4 = KVB.rearrange("p (k g d) -> p k g d", g=2, d=64)

    def replicate(blk, which=0):
        r = which * 2 + blk // 8
        lb = blk % 8
        rep, rk = REPB[r // 2], f"pb16_{r // 2}"
        pft, pfk = pf()
        for g in range(2):
            op("pe", lambda e, g=g, pft=pft: e.matmul(pft[:, :].rearrange("p (k d) -> p k d", d=64), rep[:, (r % 2) * 2 + g, :], KV4[:, lb * 8:(lb + 1) * 8, g, :],
                                                      start=(g == 0), stop=(g == 1)), r=[rk] + KVK, w=[pfk], inc=(g == 1))
        return pft, pfk

    def sample_scores():
        for blk in range(16):
            pft, pfk = replicate(blk)
            tmp, tk = P32[blk % 2], f"p32_{blk % 2}"
            t3 = tmp[:, :].rearrange("p (k d) -> p k d", d=64)
            op("dve", lambda e, pft=pft, t3=t3: e.tensor_tensor(t3, pft[:, :].rearrange("p (k d) -> p k d", d=64),
                                                               QS[:, :].unsqueeze(1).broadcast_to([128, 8, 64]), ALU.mult), r=[pfk, "qs"], w=[tk])
            op("dve", lambda e, blk=blk, t3=t3: e.tensor_reduce(SCS[:, blk * 8:(blk + 1) * 8], t3, AX.X, ALU.add), r=[tk], w=["scs"])
        op("dve", lambda e: e.tensor_tensor(OPART[:, :], KNEW[:, :], QS[:, :], ALU.mult), r=["knew", "qs"], w=["opart"])
        op("dve", lambda e: e.tensor_reduce(SCS[:, 128:129], OPART[:, :], AX.X, ALU.add), r=["opart"], w=["scs"])
        op("dve", lambda e: e.tensor_reduce(SM[:, 48:49], SCS[:, 0:129], AX.X, ALU.max, negate=True), r=["scs"], w=["snmx"])
        op("dve", lambda e: e.tensor_scalar(SM[:, 48:49], SM[:, 48:49], NSINKB[:, 8:9], None, ALU.min), r=["snmx", "nsinkb"], w=["snmx"])
        op("act", lambda e: e.activation(PS_[:, 0:129], SCS[:, 0:129], AF.Exp, bias=SM[:, 48:49], scale=1.0, accum_out=SM[:, 49:50]), r=["scs", "snmx"], w=["ps_", "ssum"])
        op("act", lambda e: e.activation(SM[:, 50:51], SINKC[:, 0:1], AF.Exp, bias=SM[:, 48:49], scale=1.0), r=["sinkc", "snmx"], w=["ses"])
        op("dve", lambda e: e.tensor_tensor(SM[:, 51:52], SM[:, 49:50], SM[:, 50:51], ALU.add), r=["ssum", "ses"], w=["sden"])
        op("dve", lambda e: e.reciprocal(SM[:, 51:52], SM[:, 51:52]), r=["sden"], w=["sden"])
        op("dve", lambda e: e.tensor_scalar(PS_[:, 0:129], PS_[:, 0:129], SM[:, 51:52], None, ALU.mult), r=["ps_", "sden"], w=["ps_"])

    def sample_pv():
        op("dve", lambda e: e.tensor_scalar(OS[:, :], VNEW[:, :], PS_[:, 128:129], None, ALU.mult), r=["vnew", "ps_"], w=["os"])
        for blk in range(16):
            pft, pfk = replicate(blk, 1)
            tmp, tk = P32[blk % 2], f"p32_{blk % 2}"
            t3 = tmp[:, :].rearrange("p (k d) -> p k d", d=64)
            op("dve", lambda e, pft=pft, t3=t3, blk=blk: e.tensor_tensor(t3, pft[:, :].rearrange("p (k d) -> p k d", d=64),
                                                                        PS_[:, blk * 8:(blk + 1) * 8].unsqueeze(2).broadcast_to([128, 8, 64]), ALU.mult),
               r=[pfk, "ps_"], w=[tk])
            op("dve", lambda e, tmp=tmp: e.tensor_reduce(OPART[:, :], tmp[:, :].rearrange("p (k d) -> p d k", d=64), AX.X, ALU.add), r=[tk], w=["opart"])
            op("dve", lambda e: e.tensor_tensor(OS[:, :], OS[:, :], OPART[:, :], ALU.add), r=["os", "opart"], w=["os"])
        for h in range(8):
            dma(lambda e, h=h: e.dma_start(out=o_s[:, h * 64:(h + 1) * 64], in_=OS[h * 16:(h + 1) * 16, :]), r=["os"], w=["o_s"])
        dma(lambda e: e.dma_start(out=ATOK[:, :], in_=o_s), r=["o_s"], w=["atok"])

    def sample_tail():
        for c in range(4):
            po, ko = pf()
            op("pe", lambda e, c=c, po=po: e.transpose(po[:, 0:NSMP], ATOK[:, c * 128:(c + 1) * 128], IDF[0:NSMP, 0:NSMP]), r=["atok", "idf"], w=[ko])
            S_ = c % 2
            op("act", lambda e, S_=S_, po=po: e.activation(ASQS[:, S_, :], po[:, 0:NSMP], AF.Square), r=[ko], w=[f"asqs{S_}"])
            op("act", lambda e, c=c, po=po: e.activation(ATTS[:, c, :], po[:, 0:NSMP], AF.Identity, scale=COLS[:, 36 + c:37 + c]), r=[ko, "cols"], w=["atts"])
            op("pe", lambda e, S_=S_, c=c: e.matmul(PSTAT[0:NSMP, 36 + c:36 + c + 1], ASQS[:, S_, :], ONES[:, 0:1], start=True, stop=True),
               r=[f"asqs{S_}", "ones"], w=[PSTK])
        wout_residual(True)
        norm_transpose(1, NSMP, 1, 8, 12, "b", smp=True, xdst=XTS)

    try:
        _gate(1)
        load_norm1(4, True, do_load=False)
        sample_state()
        for c in range(4):
            run(rg_front(4, c, True))
            run(rg_back(4, c, True))
        dma(lambda e: e.dma_start(out=sho_d, in_=ATOK[:, :]), r=["atok"])
        sample_z()
        run(prefetch_norm1(0))
        stage_wout()
        sample_scores()
        run(qkv(0))
        sample_pv()
        residual_load(0)
        prompt_mixer_rest(0, skip_qkv=True)
        sample_tail()
        prompt_ffn(0, True)
        dma(lambda e: e.dma_start(out=sco_d[:, 0:2, :], in_=sc_d[:, 512:1536].rearrange("b (j c) -> b j c", j=2)))
        dma(lambda e: e.dma_start(out=sk_d[:, 0:127, :], in_=ck_d[:, 1:128, :]))
        dma(lambda e: e.dma_start(out=sv_d[:, 0:127, :], in_=cv_d[:, 1:128, :]))
        for ti in range(1, 4):
            prompt_mixer_rest(ti, skip_qkv=(ti >= 2))
            prompt_ffn(ti, False)
    except _Stop:
        pass
    P.finish()
    P.emit()
    st.close()
    return nc


_CACHE = {}


def _host_consts():
    ident = np.eye(128, dtype=np.float32)
    i = np.arange(128)[:, None]
    j = np.arange(128)[None, :]
    prev = np.where(j >= i, 0.0, NEG).astype(np.float32)
    own = np.where(j <= i, 0.0, NEG).astype(np.float32)
    m0 = np.concatenate([prev, own], 1)
    m1 = np.concatenate([np.full((128, 128), NEG, np.float32), own], 1)
    mask = np.stack([np.concatenate([m0, m0], 1), np.concatenate([m1, m1], 1)], 1)
    rep = np.zeros((128, 8, 128), np.float32)
    for r in range(4):
        for h in range(8):
            for b in range(NSMP):
                rep[r * NSMP + b, r * 2 + h // 4, h * 16 + b] = 1.0
    return ident, np.ascontiguousarray(mask), rep


def kernel(x_prompt, x_sample, cache_k_win, cache_v_win, state_conv, state_h,
           norm1_g, w_in, conv_w, conv_b, gate_a_w, gate_a_b, gate_x_w, gate_x_b,
           lru_lambda, attn_sinks, rec_norm_g, attn_norm_g, w_out, norm2_g,
           w_gate, w_up, w_down, final_norm_g):
    f32 = lambda a: np.ascontiguousarray(np.asarray(a, dtype=np.float32))
    if "nc" not in _CACHE:
        _CACHE["nc"] = build_program()
    nc = _CACHE["nc"]
    ident, mask, rep = _host_consts()
    cols = np.zeros((128, 48), np.float32)
    cw = f32(conv_w)[0]
    for c in range(4):
        for j in range(4):
            cols[:, 4 * c + j] = cw[j, c * 128:(c + 1) * 128]
    def colset(off, vec):
        v = f32(vec).reshape(-1)
        for c in range(4):
            cols[:, off + c] = v[c * 128:(c + 1) * 128]
    colset(16, conv_b); colset(20, gate_a_b); colset(24, gate_x_b); colset(28, lru_lambda)
    colset(32, rec_norm_g); colset(36, attn_norm_g)
    gbc = np.ascontiguousarray(np.stack([np.broadcast_to(f32(g).reshape(1, D), (128, D)) for g in (norm1_g, norm2_g, final_norm_g)], 0))
    sinks = f32(attn_sinks).reshape(8)
    sinkb = np.ascontiguousarray(np.broadcast_to(sinks[None, :], (128, 8)))
    sinkc = np.ascontiguousarray(np.repeat(sinks, 16).reshape(128, 1))
    shared = {
        "w_in": f32(w_in)[0], "w_out": f32(w_out)[0], "w_gate": f32(w_gate)[0], "w_up": f32(w_up)[0], "w_down": f32(w_down)[0],
        "gate_a_w": f32(gate_a_w)[0], "gate_x_w": f32(gate_x_w)[0], "cols": cols, "gbc": gbc, "sinkb": sinkb, "sinkc": sinkc,
        "ident": ident, "mask": mask, "rep": rep,
    }
    xp = f32(x_prompt); xs = f32(x_sample)[:, 0, :]
    ck = f32(cache_k_win)[0].reshape(128, 128, 128); cv = f32(cache_v_win)[0].reshape(128, 128, 128)
    sc = f32(state_conv)[0].reshape(128, 1536); sh = f32(state_h)[0]
    in_maps = []
    for c in range(NCORES):
        sl = slice(c * NSMP, (c + 1) * NSMP)
        m = dict(shared)
        m.update({"xp": xp[c], "xs": np.ascontiguousarray(xs[sl]), "ck": np.ascontiguousarray(ck[sl]), "cv": np.ascontiguousarray(cv[sl]),
                  "sc": np.ascontiguousarray(sc[sl]), "sh": np.ascontiguousarray(sh[sl])})
        in_maps.append(m)
    res = run_bass_kernel_spmd(nc, in_maps, core_ids=list(range(NCORES)))
    R = res.results
    cat = lambda k: np.concatenate([np.asarray(r[k]) for r in R], 0)
    y_p = np.stack([np.asarray(r["y_p"]) for r in R], 0)
    y_s = cat("y_s").reshape(128, 1, D)
    p_k = np.stack([np.asarray(r["p_k"]) for r in R], 0).reshape(1, 8, 128, 2, 64)
    p_v = np.stack([np.asarray(r["p_v"]) for r in R], 0).reshape(1, 8, 128, 2, 64)
    p_c = np.stack([np.asarray(r["p_conv"]) for r in R], 0).reshape(1, 8, 3, 512)
    p_h = np.stack([np.asarray(r["p_h"]) for r in R], 0).reshape(1, 8, 512)
    s_k = cat("s_k").reshape(1, 128, 128, 2, 64)
    s_v = cat("s_v").reshape(1, 128, 128, 2, 64)
    s_c = cat("s_conv").reshape(1, 128, 3, 512)
    s_h = cat("s_h").reshape(1, 128, 512)
    out = (y_p, y_s, p_k, p_v, p_c, p_h, s_k, s_v, s_c, s_h)
    return tuple(np.ascontiguousarray(o, dtype=np.float32) for o in out)
```
